# Optimizing a Trainium2 kernel written in Bass

```python
import jax, jax.numpy as jnp
from jax import lax
import numpy as np

D_MODEL = 1024
BATCH = 4
SEQ = 8192
DEPTH = 1
DEC_BATCH = 32
DEC_SEQ = 4
PAST_LEN = 16384
PAGE_SIZE = 128

D_MIX = D_MODEL
D_LRU = D_MIX // 2
N_LRU_BLOCKS = 8
LRU_BLOCK = D_LRU // N_LRU_BLOCKS
CONV_WIDTH = 4
LRU_C = 8.0
D_ATTN = D_MIX - D_LRU
N_HEADS = 8
HEAD_DIM = D_ATTN // N_HEADS
DILATED_PATTERNS = ((128, 1), (512, 4), (2048, 16))
MAX_WINDOW = max(w for w, _ in DILATED_PATTERNS)
Q_BLOCK = 128
D_FF = -(-8 * D_MODEL // (3 * 256)) * 256
D_IN = 2 * D_LRU + 3 * D_ATTN
RMS_EPS = 1e-6

kernel_name = 'hymba_rglru_dilated_swa_step'


def rmsnorm(x, g):
    xf = x.astype(jnp.float32)
    y = xf * lax.rsqrt(jnp.mean(xf * xf, axis=-1, keepdims=True) + RMS_EPS)
    return (y * g.astype(jnp.float32)).astype(x.dtype)


def project_in(xn, w_in):
    b, t, _ = xn.shape
    proj = xn @ w_in
    u = proj[..., :D_LRU]
    gate = proj[..., D_LRU:2 * D_LRU]
    o = 2 * D_LRU
    q = proj[..., o:o + D_ATTN].reshape(b, t, N_HEADS, HEAD_DIM)
    k = proj[..., o + D_ATTN:o + 2 * D_ATTN].reshape(b, t, N_HEADS, HEAD_DIM)
    v = proj[..., o + 2 * D_ATTN:].reshape(b, t, N_HEADS, HEAD_DIM)
    return u, gate, q, k, v


def causal_depthwise_conv(u, prev, w, bias):
    t = u.shape[1]
    full = jnp.concatenate([prev.astype(u.dtype), u], axis=1)
    y = bias.astype(u.dtype)
    for j in range(CONV_WIDTH):
        y = y + full[:, j:j + t] * w[j]
    return y, full[:, -(CONV_WIDTH - 1):]


def block_diag(u, w, bias):
    b, t, _ = u.shape
    ub = u.reshape(b, t, N_LRU_BLOCKS, LRU_BLOCK)
    y = jnp.einsum('btni,nij->btnj', ub, w.astype(jnp.float32))
    return y.reshape(b, t, D_LRU) + bias.astype(jnp.float32)


def rg_lru(u, h0, w_a, b_a, w_x, b_x, lam):
    uf = u.astype(jnp.float32)
    r = jax.nn.sigmoid(block_diag(uf, w_a, b_a))
    i = jax.nn.sigmoid(block_diag(uf, w_x, b_x))
    log_a = -LRU_C * r * jax.nn.softplus(-lam.astype(jnp.float32))
    a = jnp.exp(log_a)
    bx = jnp.sqrt(-jnp.expm1(2.0 * log_a)) * (i * uf)
    bx = bx.at[:, 0].add(a[:, 0] * h0.astype(jnp.float32))

    def combine(left, right):
        a1, b1 = left
        a2, b2 = right
        return a1 * a2, a2 * b1 + b2

    _, h = lax.associative_scan(combine, (a, bx), axis=1)
    return h, h[:, -1]


def lru_group(u, gate, conv_prev, h0, conv_w, conv_b, w_a, b_a, w_x, b_x, lam):
    uc, new_conv = causal_depthwise_conv(u, conv_prev, conv_w, conv_b)
    h, h_last = rg_lru(uc, h0, w_a, b_a, w_x, b_x, lam)
    y = h * jax.nn.gelu(gate.astype(jnp.float32))
    return y.astype(u.dtype), new_conv, h_last


def dilated_attention_block(q, k_ext, v_ext, q_pos):
    scale = HEAD_DIM ** -0.5
    qf = q.astype(jnp.float32)
    outs, lses = [], []
    for window, dil in DILATED_PATTERNS:
        n_keys = window // dil + 1
        dist = jnp.arange(n_keys, dtype=jnp.int32) * dil
        idx = q_pos[:, None] - dist[None, :]
        valid = idx >= 0
        idx_c = jnp.maximum(idx, 0)
        kg = k_ext[:, idx_c].astype(jnp.float32)
        vg = v_ext[:, idx_c].astype(jnp.float32)
        s = jnp.einsum('bqhd,bqkhd->bhqk', qf, kg) * scale
        s = jnp.where(valid[None, None], s, -jnp.inf)
        lse = jax.nn.logsumexp(s, axis=-1)
        p = jnp.exp(s - lse[..., None])
        outs.append(jnp.einsum('bhqk,bqkhd->bqhd', p, vg))
        lses.append(lse)
    wts = jax.nn.softmax(jnp.stack(lses, axis=0), axis=0)
    wts = jnp.transpose(wts, (0, 1, 3, 2))[..., None]
    out = wts[0] * outs[0]
    for pi in range(1, len(DILATED_PATTERNS)):
        out = out + wts[pi] * outs[pi]
    return out


def prompt_attention(q, k, v):
    b, s, h, dh = q.shape
    n_blk = s // Q_BLOCK
    q_blocks = q.reshape(b, n_blk, Q_BLOCK, h, dh).transpose(1, 0, 2, 3, 4)
    starts = jnp.arange(n_blk, dtype=jnp.int32) * Q_BLOCK

    def one_block(args):
        qb, s0 = args
        return dilated_attention_block(qb, k, v, s0 + jnp.arange(Q_BLOCK, dtype=jnp.int32))

    o = lax.map(one_block, (q_blocks, starts))
    return o.transpose(1, 0, 2, 3, 4).reshape(b, s, h * dh)


def merge_groups(y_lru, y_attn, g_lru, g_attn, w_out):
    y = jnp.concatenate([rmsnorm(y_lru, g_lru), rmsnorm(y_attn, g_attn)], axis=-1)
    return y @ w_out


def swiglu(x, w_gate, w_up, w_down):
    return (jax.nn.silu(x @ w_gate) * (x @ w_up)) @ w_down


def setup_inputs(seed: int = 0) -> dict:
    key = jax.random.key(seed)
    ks = jax.random.split(key, 24)
    f32 = jnp.float32
    w_buf = min(MAX_WINDOW, PAST_LEN)
    nrm = lambda k, shape, s: jax.random.normal(k, shape, f32) * s
    u_a = jax.random.uniform(ks[10], (DEPTH, D_LRU), f32, minval=0.9, maxval=0.999)
    a_base = u_a ** (1.0 / LRU_C)
    lam = jnp.log(a_base) - jnp.log1p(-a_base)
    return {
        'x_prompt': nrm(ks[0], (BATCH, SEQ, D_MODEL), 1.0),
        'x_sample': nrm(ks[1], (DEC_BATCH, DEC_SEQ, D_MODEL), 1.0),
        'state_conv': nrm(ks[2], (DEPTH, DEC_BATCH, CONV_WIDTH - 1, D_LRU), 1.0),
        'state_lru': nrm(ks[3], (DEPTH, DEC_BATCH, D_LRU), 0.5),
        'cache_k': nrm(ks[4], (DEPTH, DEC_BATCH, w_buf, N_HEADS, HEAD_DIM), 1.0),
        'cache_v': nrm(ks[5], (DEPTH, DEC_BATCH, w_buf, N_HEADS, HEAD_DIM), 1.0),
        'norm_mix': 1.0 + nrm(ks[6], (DEPTH, D_MODEL), 0.01),
        'w_in': nrm(ks[7], (DEPTH, D_MODEL, D_IN), D_MODEL ** -0.5),
        'conv_w': nrm(ks[8], (DEPTH, CONV_WIDTH, D_LRU), CONV_WIDTH ** -0.5),
        'conv_b': nrm(ks[9], (DEPTH, D_LRU), 0.01),
        'lru_w_a': nrm(ks[11], (DEPTH, N_LRU_BLOCKS, LRU_BLOCK, LRU_BLOCK), LRU_BLOCK ** -0.5),
        'lru_b_a': nrm(ks[12], (DEPTH, D_LRU), 0.01),
        'lru_w_x': nrm(ks[13], (DEPTH, N_LRU_BLOCKS, LRU_BLOCK, LRU_BLOCK), LRU_BLOCK ** -0.5),
        'lru_b_x': nrm(ks[14], (DEPTH, D_LRU), 0.01),
        'lru_lambda': lam,
        'out_norm_lru': 1.0 + nrm(ks[15], (DEPTH, D_LRU), 0.01),
        'out_norm_attn': 1.0 + nrm(ks[16], (DEPTH, D_ATTN), 0.01),
        'w_out': nrm(ks[17], (DEPTH, D_MIX, D_MODEL), D_MIX ** -0.5),
        'norm_ffn': 1.0 + nrm(ks[18], (DEPTH, D_MODEL), 0.01),
        'w_gate': nrm(ks[19], (DEPTH, D_MODEL, D_FF), D_MODEL ** -0.5),
        'w_up': nrm(ks[20], (DEPTH, D_MODEL, D_FF), D_MODEL ** -0.5),
        'w_down': nrm(ks[21], (DEPTH, D_FF, D_MODEL), D_FF ** -0.5),
        'norm_final': 1.0 + nrm(ks[22], (D_MODEL,), 0.01),
    }


def reference(x_prompt, x_sample, state_conv, state_lru, cache_k, cache_v, norm_mix, w_in,
              conv_w, conv_b, lru_w_a, lru_b_a, lru_w_x, lru_b_x, lru_lambda, out_norm_lru,
              out_norm_attn, w_out, norm_ffn, w_gate, w_up, w_down, norm_final):
    xp, xs = x_prompt, x_sample
    bp, sp = xp.shape[0], xp.shape[1]
    bs, ts = xs.shape[0], xs.shape[1]
    w_p = min(MAX_WINDOW, sp)
    w_buf = cache_k.shape[2]
    conv_p, lru_p, kw_p, vw_p = [], [], [], []
    conv_s, lru_s, kn_s, vn_s = [], [], [], []
    for l in range(DEPTH):
        lru_args = (conv_w[l], conv_b[l], lru_w_a[l], lru_b_a[l], lru_w_x[l], lru_b_x[l], lru_lambda[l])
        xn = rmsnorm(xp, norm_mix[l])
        u, gate, q, k, v = project_in(xn, w_in[l])
        y_lru, c_new, h_last = lru_group(u, gate, jnp.zeros((bp, CONV_WIDTH - 1, D_LRU), xp.dtype),
                                         jnp.zeros((bp, D_LRU), jnp.float32), *lru_args)
        y_att = prompt_attention(q, k, v).astype(xp.dtype)
        xp = xp + merge_groups(y_lru, y_att, out_norm_lru[l], out_norm_attn[l], w_out[l])
        xp = xp + swiglu(rmsnorm(xp, norm_ffn[l]), w_gate[l], w_up[l], w_down[l])
        conv_p.append(c_new)
        lru_p.append(h_last.astype(xp.dtype))
        kw_p.append(k[:, sp - w_p:])
        vw_p.append(v[:, sp - w_p:])
        xn = rmsnorm(xs, norm_mix[l])
        u, gate, q, k, v = project_in(xn, w_in[l])
        y_lru, c_new, h_last = lru_group(u, gate, state_conv[l], state_lru[l], *lru_args)
        k_ext = jnp.concatenate([cache_k[l].astype(k.dtype), k], axis=1)
        v_ext = jnp.concatenate([cache_v[l].astype(v.dtype), v], axis=1)
        q_pos = w_buf + jnp.arange(ts, dtype=jnp.int32)
        y_att = dilated_attention_block(q, k_ext, v_ext, q_pos).reshape(bs, ts, D_ATTN).astype(xs.dtype)
        xs = xs + merge_groups(y_lru, y_att, out_norm_lru[l], out_norm_attn[l], w_out[l])
        xs = xs + swiglu(rmsnorm(xs, norm_ffn[l]), w_gate[l], w_up[l], w_down[l])
        conv_s.append(c_new)
        lru_s.append(h_last.astype(xs.dtype))
        kn_s.append(k)
        vn_s.append(v)
    y_prompt = rmsnorm(xp, norm_final)
    y_sample = rmsnorm(xs, norm_final)
    return (y_prompt, y_sample,
            jnp.stack(conv_p, 0), jnp.stack(lru_p, 0), jnp.stack(kw_p, 0), jnp.stack(vw_p, 0),
            jnp.stack(conv_s, 0), jnp.stack(lru_s, 0), jnp.stack(kn_s, 0), jnp.stack(vn_s, 0))
```

```python
from contextlib import ExitStack
import numpy as np
import ml_dtypes
import concourse.bass as bass
import concourse.mybir as mybir
from concourse.bass_utils import run_bass_kernel_spmd

F32 = mybir.dt.float32
BF16 = mybir.dt.bfloat16
AF = mybir.ActivationFunctionType
ALU = mybir.AluOpType

import os as _os
SAME_ENGINE_SYNC = _os.environ.get("KSES", "1") == "1"
RAW_ONLY = _os.environ.get("KRAW", "0") == "1"

D = 1024
DL = 512
NH = 8
DH = 64
DFF = 2816
NFF = 22
DIN = 2560
TOK = 4096
SUB = 512
NSUB = 8
HALO = 4096
KVH = 2048
EPS = 1e-6
VW = NH * 65


class Op:
    __slots__ = ("eng", "fn", "deps", "raw", "needs_inc", "kind", "sem", "val", "prev_slot_op", "name", "nobar")

    def __init__(self, eng, fn, kind, name=""):
        self.eng = eng
        self.fn = fn
        self.kind = kind
        self.deps = set()
        self.raw = set()
        self.needs_inc = False
        self.sem = None
        self.val = None
        self.prev_slot_op = None
        self.name = name
        self.nobar = False


class Sched:
    ENGS = ["pe", "act", "dve", "pool", "sp"]
    NSLOT = {"sp": 12, "pool": 8}

    def __init__(self):
        self.ops = {e: [] for e in self.ENGS}
        self.last_w = {}
        self.readers = {}
        self.pending = {e: set() for e in self.ENGS}
        self.bar_idx = {e: 0 for e in self.ENGS}

    def add(self, eng, fn, reads=(), writes=(), kind="compute", name="", nobar=False):
        op = Op(eng, fn, kind, name)
        op.nobar = nobar
        psr = [k for k in reads if isinstance(k, tuple) and k[0] == "ps"]
        if psr:
            reads = [k for k in reads if k not in psr]
            writes = list(writes) + [k for k in psr if k not in writes]
        for k in reads:
            w = self.last_w.get(k)
            if w is not None:
                op.deps.add(w)
                op.raw.add(w)
        for k in psr:
            w = self.last_w.get(k)
            if w is not None:
                op.raw.add(w)
        for k in writes:
            w = self.last_w.get(k)
            if w is not None:
                op.deps.add(w)
            lastc = {}
            for r in self.readers.get(k, ()):
                if r.kind == "dma":
                    op.deps.add(r)
                else:
                    lastc[r.eng] = r
            for r in lastc.values():
                op.deps.add(r)
        for k in reads:
            self.readers.setdefault(k, []).append(op)
        for k in writes:
            self.last_w[k] = op
            self.readers[k] = []
        if self.pending[eng]:
            op.deps |= self.pending[eng]
            self.pending[eng] = set()
        op.deps.discard(op)
        self.ops[eng].append(op)
        return op

    def barrier(self):
        lasts = set()
        for e in self.ENGS:
            ops = self.ops[e]
            for op in reversed(ops):
                if op.kind == "compute":
                    lasts.add(op)
                    break
            for op in ops[self.bar_idx[e]:]:
                if op.kind == "dma" and not op.nobar:
                    lasts.add(op)
            self.bar_idx[e] = len(ops)
        for e in self.ENGS:
            self.pending[e] |= lasts

    def pe(self, fn, reads=(), writes=(), **kw):
        return self.add("pe", fn, reads, writes, **kw)

    def act(self, fn, reads=(), writes=(), **kw):
        return self.add("act", fn, reads, writes, **kw)

    def dve(self, fn, reads=(), writes=(), **kw):
        return self.add("dve", fn, reads, writes, **kw)

    def pool(self, fn, reads=(), writes=(), **kw):
        return self.add("pool", fn, reads, writes, **kw)

    def dma(self, fn, reads=(), writes=(), eng="sp", **kw):
        return self.add(eng, fn, reads, writes, kind="dma", **kw)

    def emit(self, nc, stack):
        for e in self.ENGS:
            for op in self.ops[e]:
                keep = set()
                for d in op.deps:
                    if d.eng == op.eng and d.kind == "compute" and op.kind == "compute":
                        if op.eng == "pe" or not SAME_ENGINE_SYNC:
                            continue
                        if RAW_ONLY and d not in op.raw:
                            continue
                    keep.add(d)
                op.deps = keep
                for d in keep:
                    d.needs_inc = True
        sems = {}
        for e in self.ENGS:
            sems[e] = stack.enter_context(nc.semaphore("prog_" + e))
        dma_sems = {}
        for e, n in self.NSLOT.items():
            dma_sems[e] = [stack.enter_context(nc.semaphore("dma_%s_%d" % (e, i))) for i in range(n)]
        for e in self.ENGS:
            cnt = 0
            dcnt = 0
            slot_last = {}
            for op in self.ops[e]:
                if op.kind == "dma":
                    n = self.NSLOT[e]
                    slot = dcnt % n
                    rnd = dcnt // n + 1
                    op.sem = dma_sems[e][slot]
                    op.val = 16 * rnd
                    op.prev_slot_op = slot_last.get(slot)
                    slot_last[slot] = op
                    dcnt += 1
                elif op.needs_inc:
                    cnt += 1
                    op.sem = sems[e]
                    op.val = cnt
        import os
        if os.environ.get("KDEBUG"):
            for e in self.ENGS:
                ops = self.ops[e]
                print(e, "ops", len(ops), "incs", sum(1 for o in ops if o.kind == "compute" and o.needs_inc),
                      "dmas", sum(1 for o in ops if o.kind == "dma"), "maxval", max([o.val or 0 for o in ops] + [0]))
        block = stack.enter_context(nc.Block())
        sched = self

        def run(e, eng):
            waited = {}
            for op in sched.ops[e]:
                need = {}
                dl = list(op.deps)
                if op.prev_slot_op is not None:
                    dl.append(op.prev_slot_op)
                for d in dl:
                    k = d.sem
                    if need.get(k, (None, 0))[1] < d.val:
                        need[k] = (d.sem, d.val)
                for k, (s, v) in need.items():
                    if waited.get(k, 0) < v:
                        eng.wait_ge(s, v)
                        waited[k] = v
                inst = op.fn(eng)
                if op.kind == "dma":
                    inst.then_inc(op.sem, 16)
                elif op.needs_inc:
                    inst.then_inc(op.sem, 1)
            last = {}
            for op in sched.ops[e]:
                if op.kind == "dma":
                    last[op.sem] = (op.sem, op.val)
            for k, (s, v) in last.items():
                if waited.get(k, 0) < v:
                    eng.wait_ge(s, v)
                    waited[k] = v

        @block.tensor
        def _(eng):
            run("pe", eng)

        @block.scalar
        def _(eng):
            run("act", eng)

        @block.vector
        def _(eng):
            run("dve", eng)

        @block.gpsimd
        def _(eng):
            run("pool", eng)

        @block.sync
        def _(eng):
            run("sp", eng)


def _consts():
    c = {}
    c["ident"] = np.eye(128, dtype=np.float32)
    kk = np.arange(128)[:, None]
    qq = np.arange(256)[None, :]
    band = ((qq >= kk) & (qq <= kk + 128)).astype(np.float32)
    m1 = np.concatenate([band[:, 128:256], band, band, band, band[:, 0:128]], axis=1)
    m4 = np.concatenate([np.concatenate([band[:, 128:256], band[:, 0:128]], 1)] * 4, axis=1)
    qi = np.arange(32)[None, :]
    ma = (kk >= qi).astype(np.float32)
    mb = ((kk <= qi) & (kk < 32)).astype(np.float32)
    m16 = np.concatenate([ma] * 16 + [mb] * 16, axis=1)
    ii_ = np.arange(128)[None, :]
    mx = np.zeros((128, 5, 128), np.float32)
    for kb in range(5):
        jp = 128 * (kb - 4) + np.arange(128)[:, None]
        dd = ii_ - jp
        v4 = (dd >= 0) & (dd <= 128)
        v16 = (dd >= 0) & (dd % 4 == 0) & (dd <= 512)
        mx[:, kb, :] = v4.astype(np.float32) + v16.astype(np.float32)
    c["mxa"] = mx[:, 1:5, :].reshape(128, 512).astype(ml_dtypes.bfloat16)
    c["mxf"] = np.concatenate([mx[:, 0, :]] * 4, axis=1).astype(ml_dtypes.bfloat16)
    c["m1"] = m1.astype(ml_dtypes.bfloat16)
    c["m4"] = m4.astype(ml_dtypes.bfloat16)
    c["m16"] = m16.astype(ml_dtypes.bfloat16)
    ms = np.zeros((128, 9, 4), np.float32)
    for i in range(4):
        qpos = 2048 + i
        for b in range(4):
            for k in range(128):
                row = 1536 + 128 * b + k
                dist = qpos - row
                cnt = 0
                if 0 <= dist <= 128:
                    cnt += 1
                if dist % 4 == 0 and 0 <= dist <= 512:
                    cnt += 1
                ms[k, b, i] = cnt
        for k in range(128):
            ms[k, 4 + i, i] = 1.0
        for k in range(4):
            dist = i - k
            if dist >= 0:
                ms[k, 8, i] = 1.0 + (2.0 if dist == 0 else 0.0)
    c["msmp"] = ms.reshape(128, 36).astype(ml_dtypes.bfloat16)
    return c


def build_nc():
    nc = bass.Bass("TRN2", target_bir_lowering=False)
    S = Sched()

    def din(name, shape, dt=F32):
        return nc.dram_tensor(name, list(shape), dt, kind="ExternalInput").ap()

    def dout(name, shape, dt=F32):
        return nc.dram_tensor(name, list(shape), dt, kind="ExternalOutput").ap()

    def dscr(name, shape, dt=BF16):
        return nc.dram_tensor(name, list(shape), dt, kind="Internal").ap()

    xs = din("xs", [HALO + TOK, D])
    flag = din("flag", [128, 1])
    validc = din("validc", [128, 48])
    x_smp = din("x_smp", [16, D])
    sconv = din("sconv", [12, DL])
    slru = din("slru", [4, DL])
    ck = din("ck", [4, 2048, 512])
    cv = din("cv", [4, 2048, 512])
    norm_mix = din("norm_mix", [D])
    w_in = din("w_in", [D, DIN])
    conv_w = din("conv_w", [4, DL])
    conv_b = din("conv_b", [DL])
    lru_w_a = din("lru_w_a", [8, 64, 64])
    lru_b_a = din("lru_b_a", [DL])
    lru_w_x = din("lru_w_x", [8, 64, 64])
    lru_b_x = din("lru_b_x", [DL])
    lru_lambda = din("lru_lambda", [DL])
    out_norm_lru = din("out_norm_lru", [DL])
    out_norm_attn = din("out_norm_attn", [DL])
    w_out = din("w_out", [D, D])
    norm_ffn = din("norm_ffn", [D])
    w_gate = din("w_gate", [D, DFF])
    w_up = din("w_up", [D, DFF])
    w_down = din("w_down", [DFF, D])
    norm_final = din("norm_final", [1, D])
    c_ident = din("c_ident", [128, 128])
    c_m1 = din("c_m1", [128, 1024], BF16)
    c_m4 = din("c_m4", [128, 1024], BF16)
    c_m16 = din("c_m16", [128, 1024], BF16)
    c_mxa = din("c_mxa", [128, 512], BF16)
    c_mxf = din("c_mxf", [128, 512], BF16)
    c_msmp = din("c_msmp", [128, 36], BF16)

    o_y = dout("o_y", [TOK, D])
    o_ys = dout("o_ys", [16, D])
    o_convp = dout("o_convp", [3, DL])
    o_lrup = dout("o_lrup", [1, DL])
    o_kwin = dout("o_kwin", [2048, 512])
    o_vwin = dout("o_vwin", [2048, 512])
    o_convs = dout("o_convs", [12, DL])
    o_lrus = dout("o_lrus", [4, DL])
    o_knew = dout("o_knew", [16, 512])
    o_vnew = dout("o_vnew", [16, 512])
    import os
    DBG = bool(os.environ.get("KDBG"))
    if DBG:
        d_yatt = dout("d_yatt", [128, 2048], BF16)
        d_ylru = dout("d_ylru", [128, 2048], BF16)
        d_E = dout("d_E", [128, 1024], BF16)
        d_O = dout("d_O", [128, 512])

    wg_s = dscr("wg_s", [NFF, 128, 8, 128])
    wu_s = dscr("wu_s", [NFF, 128, 8, 128])
    wd_s = dscr("wd_s", [NFF, 128, D])
    Kscr = dscr("Kscr", [512, KVH + TOK])
    Vscr = dscr("Vscr", [KVH + TOK, VW])

    st = ExitStack()
    with st:
        def sb(name, shape, dt=F32):
            return st.enter_context(nc.sbuf_tensor(name, list(shape), dt))

        psum = st.enter_context(nc.psum_tensor("psum", [128, 4096], F32))

        def bank(b, n=1):
            return psum[:, b * 512:(b + n) * 512]

        def bankb(b):
            return psum[:, b * 512:(b + 1) * 512].bitcast(BF16)

        def pk(b, n=1):
            return [("ps", b + i) for i in range(n)]

        w_in_bf = sb("w_in_bf", [128, 8, DIN], BF16)
        w_out_bf = sb("w_out_bf", [128, 8, D], BF16)
        ident_f = sb("ident_f", [128, 128], F32)
        ident_b = sb("ident_b", [128, 128], BF16)
        ones_f = sb("ones_f", [128, 64], F32)
        ones_b = sb("ones_b", [128, 2], BF16)
        eps_t = sb("eps_t", [128, 1], F32)
        one_t = sb("one_t", [128, 1], F32)
        m1 = sb("m1", [128, 1024], BF16)
        mxa = sb("mxa", [128, 512], BF16)
        mxf = sb("mxf", [128, 512], BF16)
        msmp = sb("msmp", [128, 36], BF16)
        g_mix = sb("g_mix", [128, 8])
        g_ffn = sb("g_ffn", [128, 8])
        g_ol = sb("g_ol", [128, 4])
        g_oa = sb("g_oa", [128, 4])
        cw = sb("cw", [128, 4, 4])
        cb = sb("cb", [128, 4])
        b_a = sb("b_a", [128, 4])
        b_x = sb("b_x", [128, 4])
        lam = sb("lam", [128, 4])
        cvec = sb("cvec", [128, 4])
        gfin = sb("gfin", [128, D])
        flag_t = sb("flag_t", [128, 1])
        valid_t = sb("valid_t", [128, 48])
        Wa_bd = sb("Wa_bd", [128, 4, 128], BF16)
        Wx_bd = sb("Wx_bd", [128, 4, 128], BF16)
        hstate = sb("hstate", [128, 4])
        gg = sb("gg", [128, 4, SUB], BF16)
        qT = sb("qT", [128, 4, SUB], BF16)
        ylru = sb("ylru", [128, 4, SUB], BF16)
        yatt = sb("yatt", [128, 4, SUB], BF16)
        u_ext2 = [sb("u_extA", [128, 4, SUB + 3]), sb("u_extB", [128, 4, SUB + 3])]
        ssq = sb("ssq", [128, 8])
        rstd = sb("rstd", [128, 8])
        u_s = sb("u_s", [128, 4, 4, 7])
        h0_s = sb("h0_s", [128, 4, 4])
        hs_s = sb("hs_s", [128, 4, 16])

        OVL = 100 * 1024 // 2
        ovl = sb("ovl", [128, OVL], BF16)
        ovl_pos = [0]

        def ovl_reset():
            ovl_pos[0] = 0

        def ov(shape, dt=F32):
            n = int(np.prod(shape[1:]))
            nb = n * (4 if dt == F32 else 2)
            nb = (nb + 63) // 64 * 64
            a = ovl_pos[0]
            assert a + nb // 2 <= OVL, ("overlay overflow", a, nb)
            ovl_pos[0] = a + nb // 2
            v = ovl[:, a:a + (n * (2 if dt == F32 else 1))]
            if dt == F32:
                v = v.bitcast(F32)
            if len(shape) == 3:
                v = v.rearrange("p (a b) -> p a b", b=shape[2])
            elif len(shape) == 4:
                v = v.rearrange("p (a b c) -> p a b c", b=shape[2], c=shape[3])
            return v[0:shape[0]]

        def ld(out, in_, w, r=(), eng="sp", slow=False, nobar=False):
            if slow:
                S.dma(lambda e: e.dma_start(out=out, in_=in_, allow_slow_non_contiguous=True), reads=r, writes=w, eng=eng, nobar=nobar)
            else:
                S.dma(lambda e: e.dma_start(out=out, in_=in_), reads=r, writes=w, eng=eng, nobar=nobar)

        ld(ident_f[:], c_ident[:, :], ["ident_f"])
        ld(m1[:], c_m1[:, :], ["m1"])
        ld(mxa[:], c_mxa[:, :], ["mxa"])
        ld(mxf[:], c_mxf[:, :], ["mxf"])
        ld(msmp[:], c_msmp[:, :], ["msmp"])
        ld(g_mix[:], norm_mix.rearrange("(c p) -> p c", p=128), ["g_mix"], slow=True)
        ld(g_ffn[:], norm_ffn.rearrange("(c p) -> p c", p=128), ["g_ffn"], slow=True)
        ld(g_ol[:], out_norm_lru.rearrange("(c p) -> p c", p=128), ["g_ol"], slow=True)
        ld(g_oa[:], out_norm_attn.rearrange("(c p) -> p c", p=128), ["g_oa"], slow=True)
        for j in range(4):
            ld(cw[:, :, j], conv_w[j].rearrange("(c p) -> p c", p=128), [("cw", j)], slow=True)
        ld(cb[:], conv_b.rearrange("(c p) -> p c", p=128), ["cb"], slow=True)
        ld(b_a[:], lru_b_a.rearrange("(c p) -> p c", p=128), ["b_a"], slow=True)
        ld(b_x[:], lru_b_x.rearrange("(c p) -> p c", p=128), ["b_x"], slow=True)
        ld(lam[:], lru_lambda.rearrange("(c p) -> p c", p=128), ["lam"], slow=True)
        ld(gfin[:], norm_final[0:1, :].partition_broadcast(128), ["gfin"])
        ld(flag_t[:], flag[:, :], ["flag_t"])
        ld(valid_t[:], validc[:, :], ["valid_t"])
        CWK = [("cw", j) for j in range(4)]
        for c in range(4):
            for q in range(4):
                ld(u_s[:, c, q, 0:3], sconv[q * 3:(q + 1) * 3, c * 128:(c + 1) * 128].rearrange("j p -> p j"), [("u_s0", c, q)], slow=True, eng="pool")
            ld(h0_s[:, c, :], slru[:, c * 128:(c + 1) * 128].rearrange("s p -> p s"), [("h0_s", c)], slow=True, eng="pool")

        S.dve(lambda e: e.memset(ones_f[:], 1.0), writes=["ones_f"])
        S.dve(lambda e: e.memset(ones_b[:], 1.0), writes=["ones_b"])
        S.dve(lambda e: e.memset(eps_t[:], EPS), writes=["eps_t"])
        S.dve(lambda e: e.memset(one_t[:], 1.0), writes=["one_t"])
        S.dve(lambda e: e.memset(hstate[:], 0.0), writes=["hstate"])
        S.dve(lambda e: e.memset(u_ext2[0][:, :, 0:3], 0.0), writes=[("u_carry", 0)])
        S.dve(lambda e: e.tensor_copy(out=ident_b[:], in_=ident_f[:]), reads=["ident_f"], writes=["ident_b"])
        S.act(lambda e: e.activation(out=cvec[:], in_=lam[:], func=AF.Exp, scale=-1.0), reads=["lam"], writes=["cvec"])
        S.act(lambda e: e.activation(out=cvec[:], in_=cvec[:], func=AF.Ln, bias=one_t[:, 0:1], scale=1.0), reads=["cvec", "one_t"], writes=["cvec"])
        S.dve(lambda e: e.tensor_scalar(out=cvec[:], in0=cvec[:], scalar1=-8.0, scalar2=None, op0=ALU.mult), reads=["cvec"], writes=["cvec"])

        ovl_reset()
        wst = [ov([128, DIN]), ov([128, DIN])]
        for kc in range(8):
            b = kc % 2
            ld(wst[b], w_in[kc * 128:(kc + 1) * 128, :], [("wst", b)])
            if kc % 2 == 0:
                S.dve(lambda e, kc=kc, b=b: e.tensor_scalar(out=w_in_bf[:, kc, :], in0=wst[b], scalar1=g_mix[:, kc:kc + 1], scalar2=None, op0=ALU.mult),
                      reads=[("wst", b), "g_mix"], writes=[("w_in_bf", kc)])
            else:
                S.act(lambda e, kc=kc, b=b: e.activation(out=w_in_bf[:, kc, :], in_=wst[b], func=AF.Copy, scale=g_mix[:, kc:kc + 1]),
                      reads=[("wst", b), "g_mix"], writes=[("w_in_bf", kc)])
        W_IN_K = [("w_in_bf", kc) for kc in range(8)]
        for kc in range(8):
            b = kc % 2
            ld(wst[b][:, 0:D], w_out[kc * 128:(kc + 1) * 128, :], [("wst", b)])
            gsrc = g_ol[:, kc:kc + 1] if kc < 4 else g_oa[:, kc - 4:kc - 3]
            if kc % 2 == 0:
                S.dve(lambda e, kc=kc, b=b, gsrc=gsrc: e.tensor_scalar(out=w_out_bf[:, kc, :], in0=wst[b][:, 0:D], scalar1=gsrc, scalar2=None, op0=ALU.mult),
                      reads=[("wst", b), "g_ol", "g_oa"], writes=[("w_out_bf", kc)])
            else:
                S.act(lambda e, kc=kc, b=b, gsrc=gsrc: e.activation(out=w_out_bf[:, kc, :], in_=wst[b][:, 0:D], func=AF.Copy, scale=gsrc),
                      reads=[("wst", b), "g_ol", "g_oa"], writes=[("w_out_bf", kc)])
        W_OUT_K = [("w_out_bf", kc) for kc in range(8)]
        bdst = ov([128, 4, 128])
        for (wsrc, wdst, nm) in ((lru_w_a, Wa_bd, "Wa_bd"), (lru_w_x, Wx_bd, "Wx_bd")):
            S.dve(lambda e: e.memset(bdst, 0.0), writes=["bdst"])
            for c in range(4):
                ld(bdst[0:64, c, 0:64], wsrc[2 * c], ["bdst"])
                ld(bdst[64:128, c, 64:128], wsrc[2 * c + 1], ["bdst"])
            S.dve(lambda e, wdst=wdst: e.tensor_copy(out=wdst[:], in_=bdst), reads=["bdst"], writes=[nm])
        S.barrier()

        def ov_top(off_bytes, shape, dt=F32):
            n = int(np.prod(shape[1:]))
            a0 = OVL - off_bytes // 2
            v = ovl[:, a0:a0 + (n * (2 if dt == F32 else 1))]
            if dt == F32:
                v = v.bitcast(F32)
            return v[0:shape[0]]

        HW = DFF // 2
        CVB = (OVL * 2 - 36 * 1024) if os.environ.get("KCV", "mid") == "mid" else 16896
        cv_f = [ov_top(CVB, [128, HW]), ov_top(CVB - 5632, [128, HW]), ov_top(CVB - 11264, [128, HW])]
        cv_b = [ov_top(CVB - 16896, [128, HW], BF16), ov_top(CVB - 16896 - 2816, [128, HW], BF16)]
        L_OFF = 58 * 1024 // 2

        def ffn_convert():
            pieces = []
            for (wsrc, wdst) in ((w_gate, wg_s), (w_up, wu_s)):
                for kc in range(8):
                    for hf in range(2):
                        pieces.append(("gu", wsrc, wdst, kc, hf))
            for f in range(NFF):
                pieces.append(("d", f))

            def load(i):
                p = pieces[i]
                b = i % 3
                if p[0] == "gu":
                    _, wsrc, wdst, kc, hf = p
                    ld(cv_f[b], wsrc[kc * 128:(kc + 1) * 128, hf * HW:(hf + 1) * HW], [("cv_f", b)])
                else:
                    f = p[1]
                    ld(cv_f[b][:, 0:D], w_down[f * 128:(f + 1) * 128, :], [("cv_f", b)])

            def cast_store(i):
                p = pieces[i]
                fb = i % 3
                b = i % 2
                if p[0] == "gu":
                    _, wsrc, wdst, kc, hf = p
                    if b == 0:
                        S.dve(lambda e: e.tensor_scalar(out=cv_b[b], in0=cv_f[fb], scalar1=g_ffn[:, kc:kc + 1], scalar2=None, op0=ALU.mult),
                              reads=[("cv_f", fb), "g_ffn"], writes=[("cv_b", b)])
                    else:
                        S.act(lambda e: e.activation(out=cv_b[b], in_=cv_f[fb], func=AF.Copy, scale=g_ffn[:, kc:kc + 1]),
                              reads=[("cv_f", fb), "g_ffn"], writes=[("cv_b", b)])
                    ld(wdst[hf * 11:(hf + 1) * 11, :, kc, :].rearrange("f p n -> p f n"), cv_b[b].rearrange("p (f n) -> p f n", n=128),
                       ["ffn_scr"], r=[("cv_b", b)], eng="pool")
                else:
                    f = p[1]
                    if b == 0:
                        S.dve(lambda e: e.tensor_copy(out=cv_b[b][:, 0:D], in_=cv_f[fb][:, 0:D]), reads=[("cv_f", fb)], writes=[("cv_b", b)])
                    else:
                        S.act(lambda e: e.activation(out=cv_b[b][:, 0:D], in_=cv_f[fb][:, 0:D], func=AF.Copy), reads=[("cv_f", fb)], writes=[("cv_b", b)])
                    ld(wd_s[f], cv_b[b][:, 0:D], ["ffn_scr"], r=[("cv_b", b)], eng="pool")

            load(0)
            load(1)
            yield
            for i in range(len(pieces)):
                if i + 2 < len(pieces):
                    load(i + 2)
                cast_store(i)
                yield

        A_END = [0]
        LAST_UB = [0]

        def stage_A(src, row0, NT, mode, scr_col=None, win_row=None, smp=False, ub=0, split=False):
            ovl_reset()
            bs = min(128, NT)
            nb = NT // bs
            xblk = [ov([128, D]), ov([128, D])]
            xnb = [ov([128, D], BF16), ov([128, D], BF16)]
            xnT = ov([128, 8, SUB], BF16)
            junk = ov([128, D], BF16)
            kTs = ov([128, 4, SUB], BF16)
            vst = ov([128, 4, VW], BF16)
            f32st = [ov([128, 512]), ov([128, 512])]
            A_END[0] = ovl_pos[0]
            fcount = [0]
            pc1 = [0]

            def tm_proj1(blk, col0, evac):
                pb = pc1[0] % 4
                pc1[0] += 1
                for kc in range(8):
                    S.pe(lambda e, kc=kc, pb=pb: e.matmul(bank(pb)[0:bs, :], lhsT=xnT[:, kc, blk * bs:(blk + 1) * bs], rhs=w_in_bf[:, kc, col0:col0 + 512],
                                                          start=(kc == 0), stop=(kc == 7)),
                         reads=[("xnT", blk)] + W_IN_K, writes=pk(pb))
                evac(pb)

            def tm_block(blk):
                if mode not in ("own", "kv"):
                    return

                def ev(pb):
                    S.act(lambda e: e.activation(out=vst[0:bs, blk, :].rearrange("p (h d) -> p h d", d=65)[:, :, 0:64],
                                                 in_=bank(pb)[0:bs, :].rearrange("p (h d) -> p h d", d=64), func=AF.Copy),
                          reads=pk(pb), writes=[("vst", blk)])
                    if smp:
                        S.dve(lambda e: e.memset(vst[0:bs, blk, :].rearrange("p (h d) -> p h d", d=65)[:, :, 64:65], 1.0), writes=[("vst", blk, "v")])
                    else:
                        vb = scr_col // 128 + blk
                        S.dve(lambda e: e.tensor_copy(out=vst[:, blk, :].rearrange("p (h d) -> p h d", d=65)[:, :, 64:65],
                                                      in_=valid_t[:, vb:vb + 1].unsqueeze(1).to_broadcast([128, 8, 1])),
                              reads=["valid_t"], writes=[("vst", blk, "v")])
                    if win_row is not None:
                        fb = fcount[0] % 2
                        fcount[0] += 1
                        S.dve(lambda e: e.tensor_copy(out=f32st[fb][0:bs], in_=bank(pb)[0:bs, :]), reads=pk(pb), writes=[("f32st", fb)])
                        dst = (o_vnew if smp else o_vwin)
                        ld(dst[win_row + blk * bs: win_row + (blk + 1) * bs, :], f32st[fb][0:bs], ["o_v"], r=[("f32st", fb)], eng="pool")
                tm_proj1(blk, 2048, ev)
                if win_row is not None:
                    def ev2(pb):
                        fb = fcount[0] % 2
                        fcount[0] += 1
                        S.dve(lambda e: e.tensor_copy(out=f32st[fb][0:bs], in_=bank(pb)[0:bs, :]), reads=pk(pb), writes=[("f32st", fb)])
                        dst = (o_knew if smp else o_kwin)
                        ld(dst[win_row + blk * bs: win_row + (blk + 1) * bs, :], f32st[fb][0:bs], ["o_k"], r=[("f32st", fb)], eng="pool")
                    tm_proj1(blk, 1536, ev2)
                if blk == nb - 1 and not smp:
                    ld(Vscr[scr_col:scr_col + SUB, :].rearrange("(b p) c -> p b c", p=128), vst, ["Vscr"],
                       r=[("vst", k_) for k_ in range(4)] + [("vst", k_, "v") for k_ in range(4)], eng="pool")

            for blk in range(nb):
                b = blk % 2
                ld(xblk[b][0:bs], src[row0 + blk * bs: row0 + (blk + 1) * bs, :], [("xblk", b)])
                S.dve(lambda e: e.memset(ssq[0:bs, 0:1], 0.0), writes=["ssq"])
                S.act(lambda e, b=b, blk=blk: e.activation(out=junk[0:bs], in_=xblk[b][0:bs], func=AF.Square, accum_out=ssq[0:bs, 0:1]),
                      reads=[("xblk", b)], writes=["junk", "ssq"])
                S.act(lambda e: e.activation(out=ssq[0:bs, 0:1], in_=ssq[0:bs, 0:1], func=AF.Sqrt, scale=1.0 / D, bias=eps_t[0:bs, 0:1]),
                      reads=["ssq", "eps_t"], writes=["ssq"])
                S.dve(lambda e: e.reciprocal(out=rstd[0:bs, 0:1], in_=ssq[0:bs, 0:1]), reads=["ssq"], writes=["rstd"])
                S.dve(lambda e, b=b: e.tensor_scalar(out=xnb[b][0:bs], in0=xblk[b][0:bs], scalar1=rstd[0:bs, 0:1], scalar2=None, op0=ALU.mult),
                      reads=[("xblk", b), "rstd"], writes=[("xnb", b)])
                pb = 6 + b
                pv = bankb(pb)
                for kc in range(8):
                    S.pe(lambda e, kc=kc, b=b, pv=pv: e.transpose(pv[:, kc * 128: kc * 128 + bs], xnb[b][0:bs, kc * 128:(kc + 1) * 128], ident_b[0:bs, 0:bs]),
                         reads=[("xnb", b), "ident_b"], writes=pk(pb))
                S.dve(lambda e, pv=pv, blk=blk: e.tensor_copy(out=xnT[:, :, blk * bs:(blk + 1) * bs],
                                                              in_=pv.rearrange("p (k t) -> p k t", t=128)[:, :, 0:bs]),
                      reads=pk(pb), writes=[("xnT", blk)])
                if blk > 0:
                    tm_block(blk - 1)
            tm_block(nb - 1)
            res = {"xnT": xnT, "vst": vst}

            def part2():
                XNT = [("xnT", blk) for blk in range(nb)]
                pcount = [0]

                def fm_proj(col0, evac):
                    pb = pcount[0] % 4
                    pcount[0] += 1
                    for kc in range(8):
                        S.pe(lambda e, kc=kc, pb=pb: e.matmul(bank(pb)[:, 0:NT], lhsT=w_in_bf[:, kc, col0:col0 + 128], rhs=xnT[:, kc, 0:NT],
                                                              start=(kc == 0), stop=(kc == 7)),
                             reads=XNT + W_IN_K, writes=pk(pb))
                    evac(pb)

                def tm_proj(blk, col0, evac):
                    pb = pcount[0] % 4
                    pcount[0] += 1
                    for kc in range(8):
                        S.pe(lambda e, kc=kc, pb=pb: e.matmul(bank(pb)[0:bs, :], lhsT=xnT[:, kc, blk * bs:(blk + 1) * bs], rhs=w_in_bf[:, kc, col0:col0 + 512],
                                                              start=(kc == 0), stop=(kc == 7)),
                             reads=XNT + W_IN_K, writes=pk(pb))
                    evac(pb)

                for c in range(4):
                    if smp:
                        def ev(pb, c=c):
                            S.act(lambda e: e.activation(out=u_s[:, c, :, 3:7], in_=bank(pb)[:, 0:16].rearrange("p (s t) -> p s t", t=4), func=AF.Copy),
                                  reads=pk(pb), writes=[("u_s", c)])
                    else:
                        def ev(pb, c=c):
                            S.act(lambda e: e.activation(out=u_ext2[ub][:, c, 3:3 + SUB], in_=bank(pb)[:, 0:SUB], func=AF.Copy),
                                  reads=pk(pb), writes=[("u_ext", ub, c)])
                    fm_proj(c * 128, ev)
                if not smp:
                    for c in range(4):
                        L1_pre(c, ub)
                if mode == "own":
                    for c in range(4):
                        def ev(pb, c=c):
                            S.act(lambda e: e.activation(out=gg[:, c, 0:NT], in_=bank(pb)[:, 0:NT], func=AF.Gelu_apprx_tanh),
                                  reads=pk(pb), writes=[("gg", c)])
                        fm_proj(512 + c * 128, ev)
                        if not smp:
                            L1_post(c, ub)
                    for c in range(4):
                        def ev(pb, c=c):
                            S.dve(lambda e: e.tensor_copy(out=qT[:, c, 0:NT], in_=bank(pb)[:, 0:NT]), reads=pk(pb), writes=[("qT", c)])
                        fm_proj(1024 + c * 128, ev)
                elif not smp:
                    for c in range(4):
                        L1_post(c, ub)
                if mode in ("own", "kv"):
                    res["kTs"] = kTs
                    for c in range(4):
                        def ev(pb, c=c):
                            S.dve(lambda e: e.tensor_copy(out=kTs[:, c, 0:NT], in_=bank(pb)[:, 0:NT]), reads=pk(pb), writes=[("kTs", c)])
                        fm_proj(1536 + c * 128, ev)
                        if not smp:
                            ld(Kscr[c * 128:(c + 1) * 128, scr_col:scr_col + SUB], kTs[:, c, :], ["Kscr"], r=[("kTs", c)], eng="pool")
                return res

            if split:
                return part2
            return part2()

        def L_bufs():
            ovl_pos[0] = L_OFF
            d = {}
            d["uc"] = ov([128, 4, SUB])
            d["ucb"] = ov([128, 4, SUB], BF16)
            d["rr"] = ov([128, 4, SUB])
            d["ii"] = ov([128, 4, SUB])
            d["tmp"] = [ov([128, SUB]) for _ in range(4)]
            d["hb"] = [ov([128, SUB]), ov([128, SUB])]
            return d

        def L1_pre(c, ub):
            d = L_bufs()
            uc, ucb = d["uc"], d["ucb"]
            u_ext = u_ext2[ub]
            S.dve(lambda e: e.tensor_scalar(out=uc[:, c, :], in0=u_ext[:, c, 3:3 + SUB], scalar1=cw[:, c, 3:4], scalar2=cb[:, c:c + 1], op0=ALU.mult, op1=ALU.add),
                  reads=[("u_ext", ub, c), ("u_carry", ub), "cb"] + CWK, writes=[("uc", c)])
            for j in (2, 1, 0):
                S.dve(lambda e, j=j: e.scalar_tensor_tensor(out=uc[:, c, :], in0=u_ext[:, c, j:j + SUB], scalar=cw[:, c, j:j + 1], in1=uc[:, c, :], op0=ALU.mult, op1=ALU.add),
                      reads=[("u_ext", ub, c), ("u_carry", ub), ("uc", c)] + CWK, writes=[("uc", c)])
            S.act(lambda e: e.activation(out=ucb[:, c, :], in_=uc[:, c, :], func=AF.Copy), reads=[("uc", c)], writes=[("ucb", c)])

        def L1_post(c, ub):
            d = L_bufs()
            ucb, rr, ii = d["ucb"], d["rr"], d["ii"]
            S.pe(lambda e: e.matmul(bank(4)[:, 0:SUB], lhsT=Wa_bd[:, c, :], rhs=ucb[:, c, :], start=True, stop=True), reads=[("ucb", c), "Wa_bd"], writes=pk(4))
            S.pe(lambda e: e.matmul(bank(5)[:, 0:SUB], lhsT=Wx_bd[:, c, :], rhs=ucb[:, c, :], start=True, stop=True), reads=[("ucb", c), "Wx_bd"], writes=pk(5))
            S.act(lambda e: e.activation(out=rr[:, c, :], in_=bank(4)[:, 0:SUB], func=AF.Sigmoid, bias=b_a[:, c:c + 1], scale=1.0), reads=pk(4) + ["b_a"], writes=[("rr", c)])
            S.act(lambda e: e.activation(out=ii[:, c, :], in_=bank(5)[:, 0:SUB], func=AF.Sigmoid, bias=b_x[:, c:c + 1], scale=1.0), reads=pk(5) + ["b_x"], writes=[("ii", c)])

        def L2_gen(own, last=False, ub=0):
            d = L_bufs()
            uc, rr, ii, tmp, hb = d["uc"], d["rr"], d["ii"], d["tmp"], d["hb"]
            u_ext = u_ext2[ub]
            for c in range(4):
                S.act(lambda e, c=c: e.activation(out=rr[:, c, :], in_=rr[:, c, :], func=AF.Exp, scale=cvec[:, c:c + 1]), reads=[("rr", c), "cvec"], writes=[("rr", c)])
                yield
            for c in range(4):
                S.act(lambda e, c=c: e.activation(out=tmp[c], in_=rr[:, c, :], func=AF.Square), reads=[("rr", c)], writes=[("tmp", c)])
            for c in range(4):
                S.act(lambda e, c=c: e.activation(out=tmp[c], in_=tmp[c], func=AF.Sqrt, scale=-1.0, bias=one_t[:, 0:1]), reads=[("tmp", c), "one_t"], writes=[("tmp", c)])
            yield
            for c in range(4):
                t = tmp[c]
                tk = ("tmp", c)
                h = hb[c % 2]
                hk = ("hb", c % 2)
                S.dve(lambda e, c=c, t=t: e.tensor_tensor(out=ii[:, c, :], in0=ii[:, c, :], in1=t, op=ALU.mult), reads=[("ii", c), tk], writes=[("ii", c)])
                yield
                S.dve(lambda e, c=c: e.tensor_tensor(out=ii[:, c, :], in0=ii[:, c, :], in1=uc[:, c, :], op=ALU.mult), reads=[("ii", c), ("uc", c)], writes=[("ii", c)])
                yield
                S.dve(lambda e, c=c, h=h: e.tensor_tensor_scan(out=h, data0=rr[:, c, :], data1=ii[:, c, :], initial=hstate[:, c:c + 1], op0=ALU.mult, op1=ALU.add),
                      reads=[("rr", c), ("ii", c), "hstate"], writes=[hk])
                S.dve(lambda e, c=c, h=h: e.tensor_copy(out=hstate[:, c:c + 1], in_=h[:, SUB - 1:SUB]), reads=[hk], writes=["hstate"])
                yield
                if own:
                    S.pool(lambda e, c=c, h=h: e.tensor_tensor(out=ylru[:, c, :], in0=h, in1=gg[:, c, :], op=ALU.mult), reads=[hk, ("gg", c)], writes=[("ylru", c)])
            if last:
                LAST_UB[0] = ub
            S.dve(lambda e: e.tensor_copy(out=u_ext2[1 - ub][:, :, 0:3], in_=u_ext[:, :, SUB:SUB + 3]),
                  reads=[("u_ext", ub, c) for c in range(4)], writes=[("u_carry", 1 - ub)])
            yield

        def finalize_head(h, ob, NTq, otmp, rz):
            c = h // 2
            po = (h % 2) * 64
            S.dve(lambda e: e.reciprocal(out=rz[64:65, 0:NTq], in_=bank(ob)[64:65, 0:NTq]), reads=pk(ob), writes=["rz"])
            S.dve(lambda e: e.tensor_copy(out=otmp[0:64, 0:NTq], in_=bank(ob)[0:64, 0:NTq]), reads=pk(ob), writes=["otmp"])
            S.pe(lambda e: e.matmul(bank(6)[0:64, 0:NTq], lhsT=ones_f[64:65, 0:64], rhs=rz[64:65, 0:NTq], start=True, stop=True),
                 reads=["rz", "ones_f"], writes=pk(6))
            S.dve(lambda e: e.tensor_tensor(out=yatt[po:po + 64, c, 0:NTq], in0=otmp[0:64, 0:NTq], in1=bank(6)[0:64, 0:NTq], op=ALU.mult),
                  reads=["otmp"] + pk(6), writes=[("yatt", h)])

        def stage_B(s, hook=None):
            ovl_reset()
            KT = ov([128, 4, KVH + SUB], BF16)
            V1 = ov([128, 5, VW], BF16)
            V4x = ov([128, 20, VW], BF16)
            E1 = ov([128, 1024], BF16)
            Ex = [ov([128, 512], BF16) for _ in range(4)]
            EF = ov([128, 512], BF16)
            otmp = ov([128, SUB])
            rz = ov([128, SUB])
            c0 = s * SUB
            ld(KT[:, 0, :], Kscr[0:128, c0:c0 + KVH + SUB], [("KT", 0)], r=["Kscr"])
            R0 = KVH + s * SUB
            ld(V1, Vscr[R0 - 128:R0 + 512, :].rearrange("(b p) c -> p b c", p=128), ["V1"], r=["Vscr"])
            for r in range(4):
                src = Vscr[R0 - 2048:R0 + 512, :].rearrange("(b j q) c -> q j b c", b=5, q=4)[r]
                ld(V4x[:, 5 * r:5 * r + 5, :], src, [("V4x", r)], r=["Vscr"])
            for c in range(1, 4):
                ld(KT[:, c, :], Kscr[c * 128:(c + 1) * 128, c0:c0 + KVH + SUB], [("KT", c)], r=["Kscr"])
            WQ = KVH

            def hv(h):
                c = h // 2
                po = (h % 2) * 64
                return c, po, qT[po:po + 64, c, :], KT[po:po + 64, c, :], 4 + (h % 2), slice(h * 65, (h + 1) * 65), [("qT", c)]

            def tiles1():
                col = 0
                out = []
                for kb in range(5):
                    q0 = max(0, 128 * (kb - 1))
                    q1 = min(SUB, 128 * (kb + 1))
                    out.append((kb, q0, q1, col))
                    col += q1 - q0
                return out

            def QK1(h):
                c, po, qh, kh, ob, hs, qk = hv(h)
                for (kb, q0, q1, col) in tiles1():
                    kcol = WQ - 128 + 128 * kb
                    for qq in range(q0, q1, 128):
                        cc = col + (qq - q0)
                        S.pe(lambda e, kcol=kcol, qq=qq, cc=cc: e.matmul(psum[:, cc:cc + 128], lhsT=kh[:, kcol:kcol + 128], rhs=qh[:, qq:qq + 128], start=True, stop=True),
                             reads=[("KT", c)] + qk, writes=pk(0, 2))

            def X1(h):
                for hb_ in range(2):
                    S.act(lambda e, hb_=hb_: e.activation(out=E1[:, hb_ * 512:(hb_ + 1) * 512], in_=bank(hb_), func=AF.Exp, scale=0.125),
                          reads=pk(hb_), writes=[("E1", hb_)])
                    S.dve(lambda e, hb_=hb_: e.tensor_tensor(out=E1[:, hb_ * 512:(hb_ + 1) * 512], in0=E1[:, hb_ * 512:(hb_ + 1) * 512], in1=m1[:, hb_ * 512:(hb_ + 1) * 512], op=ALU.mult),
                          reads=[("E1", hb_), "m1"], writes=[("E1", hb_)])

            def PV1(h):
                c, po, qh, kh, ob, hs, qk = hv(h)
                for (kb, q0, q1, col) in tiles1():
                    for qt in range(q0 // 128, q1 // 128):
                        pc = col + (qt * 128 - q0)
                        S.pe(lambda e, kb=kb, qt=qt, pc=pc: e.matmul(bank(ob)[0:65, qt * 128:(qt + 1) * 128], lhsT=V1[:, kb, hs], rhs=E1[:, pc:pc + 128],
                                                                     start=(kb == 0), stop=False, skip_group_check=True),
                             reads=["V1", ("E1", 0), ("E1", 1)], writes=pk(ob))

            def QKx(h, r):
                c, po, qh, kh, ob, hs, qk = hv(h)
                bx_ = 2 + (r % 2)
                for kb in range(5):
                    kcol = 512 * kb + r
                    if kb == 0:
                        out = bank(6)[:, r * 128:(r + 1) * 128]
                        wk = pk(6)
                    else:
                        out = bank(bx_)[:, (kb - 1) * 128:kb * 128]
                        wk = pk(bx_)
                    S.pe(lambda e, kcol=kcol, out=out: e.matmul(out, lhsT=kh[:, kcol:kcol + 509:4], rhs=qh[:, r:SUB:4], start=True, stop=True),
                         reads=[("KT", c)] + qk, writes=wk)

            def Xx(h, r):
                bx_ = 2 + (r % 2)
                S.act(lambda e: e.activation(out=Ex[r][:, :], in_=bank(bx_), func=AF.Exp, scale=0.125), reads=pk(bx_), writes=[("Ex", r)])
                S.dve(lambda e: e.tensor_tensor(out=Ex[r][:, :], in0=Ex[r][:, :], in1=mxa[:, :], op=ALU.mult), reads=[("Ex", r), "mxa"], writes=[("Ex", r)])

            def XF(h):
                S.act(lambda e: e.activation(out=EF[:, :], in_=bank(6), func=AF.Exp, scale=0.125), reads=pk(6), writes=["EF"])
                S.dve(lambda e: e.tensor_tensor(out=EF[:, :], in0=EF[:, :], in1=mxf[:, :], op=ALU.mult), reads=["EF", "mxf"], writes=["EF"])

            def PVx(h, r):
                c, po, qh, kh, ob, hs, qk = hv(h)
                for kb in range(1, 5):
                    S.pe(lambda e, kb=kb: e.matmul(bank(ob)[0:65, r:SUB:4], lhsT=V4x[:, 5 * r + kb, hs], rhs=Ex[r][:, (kb - 1) * 128:kb * 128],
                                                   start=False, stop=False, skip_group_check=True),
                         reads=[("V4x", r), ("Ex", r)], writes=pk(ob))

            def PVF(h):
                c, po, qh, kh, ob, hs, qk = hv(h)
                for r in range(4):
                    S.pe(lambda e, r=r: e.matmul(bank(ob)[0:65, r:SUB:4], lhsT=V4x[:, 5 * r, hs], rhs=EF[:, r * 128:(r + 1) * 128],
                                                 start=False, stop=(r == 3), skip_group_check=True),
                         reads=[("V4x", r), "EF"], writes=pk(ob))

            def FIN_a(h):
                c, po, qh, kh, ob, hs, qk = hv(h)
                S.dve(lambda e: e.reciprocal(out=rz[64:65, 0:SUB], in_=bank(ob)[64:65, 0:SUB]), reads=pk(ob), writes=["rz"])
                S.act(lambda e: e.activation(out=otmp[0:64, 0:SUB], in_=bank(ob)[0:64, 0:SUB], func=AF.Copy), reads=pk(ob), writes=["otmp"])

            def FIN_b(h):
                c, po, qh, kh, ob, hs, qk = hv(h)
                S.pe(lambda e: e.matmul(bank(7)[0:64, 0:SUB], lhsT=ones_f[64:65, 0:64], rhs=rz[64:65, 0:SUB], start=True, stop=True),
                     reads=["rz", "ones_f"], writes=pk(7))
                S.dve(lambda e: e.tensor_tensor(out=yatt[po:po + 64, c, 0:SUB], in0=otmp[0:64, 0:SUB], in1=bank(7)[0:64, 0:SUB], op=ALU.mult),
                      reads=["otmp"] + pk(7), writes=[("yatt", h)])

            QK1(0); X1(0)
            for h in range(NH):
                hk_ = hook if hook else (lambda: None)
                QKx(h, 0); Xx(h, 0); hk_(); QKx(h, 1); Xx(h, 1)
                if h > 0:
                    FIN_b(h - 1)
                hk_()
                PV1(h)
                QKx(h, 2); Xx(h, 2); hk_(); QKx(h, 3); Xx(h, 3); XF(h)
                hk_()
                if h + 1 < NH:
                    QK1(h + 1); X1(h + 1)
                for r in range(4):
                    PVx(h, r)
                PVF(h)
                FIN_a(h)
            FIN_b(NH - 1)

        def stage_C(src, row0, NT, dst, drow0):
            ovl_reset()
            bs = min(128, NT)
            nb = NT // bs
            xblk = [ov([128, D]), ov([128, D])]
            x1 = ov([128, 4, D])
            xn2T = ov([128, 8, SUB], BF16)
            hmid = ov([128, NFF, SUB], BF16)
            ysl = ov([128, 4, SUB], BF16)
            ysa = ov([128, 4, SUB], BF16)
            wgr = [ov([128, 8, 128], BF16) for _ in range(3)]
            wur = [ov([128, 8, 128], BF16) for _ in range(3)]
            wdr = [ov([128, D], BF16) for _ in range(4)]
            sg = [ov([128, SUB]), ov([128, SUB])]
            xnb = [ov([128, D], BF16), ov([128, D], BF16)]
            junk = ov([128, D], BF16)
            YL = [("ylru", c) for c in range(4)]
            YA = [("yatt", h) for h in range(8)]
            S.act(lambda e: e.activation(out=ysl[:, :, 0:NT], in_=ylru[:, :, 0:NT], func=AF.Square), reads=YL, writes=["ysl"])
            S.dve(lambda e: e.tensor_tensor(out=ysa[:, :, 0:NT], in0=yatt[:, :, 0:NT], in1=yatt[:, :, 0:NT], op=ALU.mult), reads=YA, writes=["ysa"])
            def c_tr(blk):
                b = blk % 2
                ts = slice(blk * bs, (blk + 1) * bs)
                pb = 4 + b
                pv = bankb(pb)
                for kc in range(8):
                    S.pe(lambda e, kc=kc: e.transpose(pv[:, kc * 128: kc * 128 + bs], xnb[b][0:bs, kc * 128:(kc + 1) * 128], ident_b[0:bs, 0:bs]),
                         reads=[("xnb", b), "ident_b"], writes=pk(pb))
                S.act(lambda e: e.activation(out=xn2T[:, :, ts], in_=pv.rearrange("p (k t) -> p k t", t=128)[:, :, 0:bs], func=AF.Copy),
                      reads=pk(pb), writes=[("xn2T", blk)])

            for blk in range(nb):
                b = blk % 2
                ts = slice(blk * bs, (blk + 1) * bs)
                ld(xblk[b][0:bs], src[row0 + blk * bs: row0 + (blk + 1) * bs, :], [("xblk", b)])
                for c in range(4):
                    S.pe(lambda e, c=c, ts=ts: e.matmul(bank(6)[0:bs, 0:1], lhsT=ysl[:, c, ts], rhs=ones_b[:, 0:1], start=(c == 0), stop=(c == 3), skip_group_check=True),
                         reads=["ysl", "ones_b"], writes=pk(6))
                for c in range(4):
                    S.pe(lambda e, c=c, ts=ts: e.matmul(bank(6)[0:bs, 1:2], lhsT=ysa[:, c, ts], rhs=ones_b[:, 0:1], start=(c == 0), stop=(c == 3), skip_group_check=True),
                         reads=["ysa", "ones_b"], writes=pk(6))
                S.act(lambda e: e.activation(out=ssq[0:bs, 2:4], in_=bank(6)[0:bs, 0:2], func=AF.Sqrt, scale=1.0 / DL, bias=eps_t[0:bs, 0:1]),
                      reads=pk(6) + ["eps_t"], writes=["ssq2"])
                S.dve(lambda e: e.reciprocal(out=rstd[0:bs, 2:4], in_=ssq[0:bs, 2:4]), reads=["ssq2"], writes=["rstd2"])
                for half in range(2):
                    for c in range(4):
                        S.pe(lambda e, c=c, ts=ts, half=half: e.matmul(bank(half)[0:bs, :], lhsT=ylru[:, c, ts], rhs=w_out_bf[:, c, half * 512:(half + 1) * 512],
                                                                       start=(c == 0), stop=(c == 3)),
                             reads=YL + W_OUT_K, writes=pk(half))
                for half in range(2):
                    for c in range(4):
                        S.pe(lambda e, c=c, ts=ts, half=half: e.matmul(bank(2 + half)[0:bs, :], lhsT=yatt[:, c, ts], rhs=w_out_bf[:, 4 + c, half * 512:(half + 1) * 512],
                                                                       start=(c == 0), stop=(c == 3)),
                             reads=YA + W_OUT_K, writes=pk(2 + half))
                S.dve(lambda e, b=b, blk=blk: e.scalar_tensor_tensor(out=x1[0:bs, blk, :], in0=psum[0:bs, 0:1024], scalar=rstd[0:bs, 2:3], in1=xblk[b][0:bs],
                                                                     op0=ALU.mult, op1=ALU.add),
                      reads=pk(0, 2) + ["rstd2", ("xblk", b)], writes=[("x1", blk)])
                S.dve(lambda e, blk=blk: e.scalar_tensor_tensor(out=x1[0:bs, blk, :], in0=psum[0:bs, 1024:2048], scalar=rstd[0:bs, 3:4], in1=x1[0:bs, blk, :],
                                                                op0=ALU.mult, op1=ALU.add),
                      reads=pk(2, 2) + ["rstd2", ("x1", blk)], writes=[("x1", blk)])
                S.dve(lambda e: e.memset(ssq[0:bs, 4:5], 0.0), writes=["ssq3"])
                S.act(lambda e, blk=blk: e.activation(out=junk[0:bs], in_=x1[0:bs, blk, :], func=AF.Square, accum_out=ssq[0:bs, 4:5]),
                      reads=[("x1", blk)], writes=["junk", "ssq3"])
                S.act(lambda e: e.activation(out=ssq[0:bs, 4:5], in_=ssq[0:bs, 4:5], func=AF.Sqrt, scale=1.0 / D, bias=eps_t[0:bs, 0:1]),
                      reads=["ssq3", "eps_t"], writes=["ssq3"])
                S.dve(lambda e: e.reciprocal(out=rstd[0:bs, 4:5], in_=ssq[0:bs, 4:5]), reads=["ssq3"], writes=["rstd3"])
                S.dve(lambda e, b=b, blk=blk: e.tensor_scalar(out=xnb[b][0:bs], in0=x1[0:bs, blk, :], scalar1=rstd[0:bs, 4:5], scalar2=None, op0=ALU.mult),
                      reads=[("x1", blk), "rstd3"], writes=[("xnb", b)])
                if blk > 0:
                    c_tr(blk - 1)
            c_tr(nb - 1)
            XN2 = [("xn2T", blk) for blk in range(nb)]
            for f in range(NFF):
                rb = f % 3
                ld(wgr[rb], wg_s[f], [("wgr", rb)], r=["ffn_scr"])
                ld(wur[rb], wu_s[f], [("wur", rb)], r=["ffn_scr"])
                pg = (f % 2) * 2
                for kc in range(8):
                    S.pe(lambda e, kc=kc, rb=rb, pg=pg: e.matmul(bank(pg)[:, 0:NT], lhsT=wgr[rb][:, kc, :], rhs=xn2T[:, kc, 0:NT], start=(kc == 0), stop=(kc == 7)),
                         reads=XN2 + [("wgr", rb)], writes=pk(pg))
                for kc in range(8):
                    S.pe(lambda e, kc=kc, rb=rb, pg=pg: e.matmul(bank(pg + 1)[:, 0:NT], lhsT=wur[rb][:, kc, :], rhs=xn2T[:, kc, 0:NT], start=(kc == 0), stop=(kc == 7)),
                         reads=XN2 + [("wur", rb)], writes=pk(pg + 1))
                sgb = sg[f % 2]
                S.act(lambda e, pg=pg, sgb=sgb: e.activation(out=sgb[:, 0:NT], in_=bank(pg)[:, 0:NT], func=AF.Silu), reads=pk(pg), writes=[("sg", f % 2)])
                S.dve(lambda e, f=f, pg=pg, sgb=sgb: e.tensor_tensor(out=hmid[:, f, 0:NT], in0=sgb[:, 0:NT], in1=bank(pg + 1)[:, 0:NT], op=ALU.mult),
                      reads=[("sg", f % 2)] + pk(pg + 1), writes=[("hmid", f)])
            HM = [("hmid", f) for f in range(NFF)]
            dcnt = 0
            for p0 in range(0, nb, 2):
                blks = list(range(p0, min(nb, p0 + 2)))
                for f in range(NFF):
                    rb = dcnt % 4
                    dcnt += 1
                    ld(wdr[rb], wd_s[f], [("wdr", rb)], r=["ffn_scr"])
                    for bi, blk in enumerate(blks):
                        ts = slice(blk * bs, (blk + 1) * bs)
                        for half in range(2):
                            pb = bi * 2 + half
                            S.pe(lambda e, f=f, ts=ts, half=half, pb=pb, rb=rb: e.matmul(bank(pb)[0:bs, :], lhsT=hmid[:, f, ts], rhs=wdr[rb][:, half * 512:(half + 1) * 512],
                                                                                         start=(f == 0), stop=(f == NFF - 1)),
                                 reads=HM + [("wdr", rb)], writes=pk(pb))
                for bi, blk in enumerate(blks):
                    b = blk % 2
                    yf = xblk[b]
                    S.dve(lambda e, bi=bi, blk=blk, yf=yf: e.tensor_tensor(out=yf[0:bs], in0=psum[0:bs, bi * 1024:(bi + 1) * 1024], in1=x1[0:bs, blk, :], op=ALU.add),
                          reads=pk(bi * 2, 2) + [("x1", blk)], writes=[("xblk", b)])
                    S.dve(lambda e: e.memset(ssq[0:bs, 5:6], 0.0), writes=["ssq4"])
                    S.act(lambda e, yf=yf: e.activation(out=junk[0:bs], in_=yf[0:bs], func=AF.Square, accum_out=ssq[0:bs, 5:6]),
                          reads=[("xblk", b)], writes=["junk", "ssq4"])
                    S.act(lambda e: e.activation(out=ssq[0:bs, 5:6], in_=ssq[0:bs, 5:6], func=AF.Sqrt, scale=1.0 / D, bias=eps_t[0:bs, 0:1]),
                          reads=["ssq4", "eps_t"], writes=["ssq4"])
                    S.dve(lambda e: e.reciprocal(out=rstd[0:bs, 5:6], in_=ssq[0:bs, 5:6]), reads=["ssq4"], writes=["rstd4"])
                    S.dve(lambda e, yf=yf: e.scalar_tensor_tensor(out=yf[0:bs], in0=yf[0:bs], scalar=rstd[0:bs, 5:6], in1=gfin[0:bs], op0=ALU.mult, op1=ALU.mult),
                          reads=[("xblk", b), "rstd4", "gfin"], writes=[("xblk", b)])
                    ld(dst[drow0 + blk * bs: drow0 + (blk + 1) * bs, :], yf[0:bs], ["o_y"], r=[("xblk", b)], eng="pool", nobar=(NT == SUB))


        def stage_L_smp():
            ucs = ov([128, 4, 16])
            ucb = ov([128, 4, 16], BF16)
            rr = ov([128, 4, 16])
            ii = ov([128, 4, 16])
            tmp = ov([128, 4, 16])
            hs = hs_s
            for c in range(4):
                uv = ucs[:, c, :].rearrange("p (s t) -> p s t", t=4)
                S.dve(lambda e, c=c, uv=uv: e.tensor_scalar(out=uv, in0=u_s[:, c, :, 3:7], scalar1=cw[:, c, 3:4], scalar2=cb[:, c:c + 1], op0=ALU.mult, op1=ALU.add),
                      reads=[("u_s", c), "cb"] + [("u_s0", c, q) for q in range(4)] + CWK, writes=[("ucs", c)])
                for j in (2, 1, 0):
                    S.dve(lambda e, c=c, j=j, uv=uv: e.scalar_tensor_tensor(out=uv, in0=u_s[:, c, :, j:j + 4], scalar=cw[:, c, j:j + 1], in1=uv, op0=ALU.mult, op1=ALU.add),
                          reads=[("u_s", c), ("ucs", c)] + [("u_s0", c, q) for q in range(4)] + CWK, writes=[("ucs", c)])
                S.act(lambda e, c=c: e.activation(out=ucb[:, c, :], in_=ucs[:, c, :], func=AF.Copy), reads=[("ucs", c)], writes=[("ucbs", c)])
                S.pe(lambda e, c=c: e.matmul(bank(4)[:, 0:16], lhsT=Wa_bd[:, c, :], rhs=ucb[:, c, :], start=True, stop=True), reads=[("ucbs", c), "Wa_bd"], writes=pk(4))
                S.pe(lambda e, c=c: e.matmul(bank(5)[:, 0:16], lhsT=Wx_bd[:, c, :], rhs=ucb[:, c, :], start=True, stop=True), reads=[("ucbs", c), "Wx_bd"], writes=pk(5))
                S.act(lambda e, c=c: e.activation(out=rr[:, c, :], in_=bank(4)[:, 0:16], func=AF.Sigmoid, bias=b_a[:, c:c + 1], scale=1.0), reads=pk(4) + ["b_a"], writes=[("rrs", c)])
                S.act(lambda e, c=c: e.activation(out=ii[:, c, :], in_=bank(5)[:, 0:16], func=AF.Sigmoid, bias=b_x[:, c:c + 1], scale=1.0), reads=pk(5) + ["b_x"], writes=[("iis", c)])
            for c in range(4):
                S.act(lambda e, c=c: e.activation(out=rr[:, c, :], in_=rr[:, c, :], func=AF.Exp, scale=cvec[:, c:c + 1]), reads=[("rrs", c), "cvec"], writes=[("rrs", c)])
            for c in range(4):
                S.act(lambda e, c=c: e.activation(out=tmp[:, c, :], in_=rr[:, c, :], func=AF.Square), reads=[("rrs", c)], writes=[("tmps", c)])
                S.act(lambda e, c=c: e.activation(out=tmp[:, c, :], in_=tmp[:, c, :], func=AF.Sqrt, scale=-1.0, bias=one_t[:, 0:1]), reads=[("tmps", c), "one_t"], writes=[("tmps", c)])
                S.dve(lambda e, c=c: e.tensor_tensor(out=ii[:, c, :], in0=ii[:, c, :], in1=tmp[:, c, :], op=ALU.mult), reads=[("iis", c), ("tmps", c)], writes=[("iis", c)])
                S.dve(lambda e, c=c: e.tensor_tensor(out=ii[:, c, :], in0=ii[:, c, :], in1=ucs[:, c, :], op=ALU.mult), reads=[("iis", c), ("ucs", c)], writes=[("iis", c)])
                for q in range(4):
                    S.dve(lambda e, c=c, q=q: e.tensor_tensor_scan(out=hs[:, c, q * 4:(q + 1) * 4], data0=rr[:, c, q * 4:(q + 1) * 4], data1=ii[:, c, q * 4:(q + 1) * 4],
                                                                   initial=h0_s[:, c, q:q + 1], op0=ALU.mult, op1=ALU.add),
                          reads=[("rrs", c), ("iis", c), ("h0_s", c)], writes=[("hs", c, q)])
                HK = [("hs", c, q) for q in range(4)]
                S.dve(lambda e, c=c: e.tensor_tensor(out=ylru[:, c, 0:16], in0=hs[:, c, :], in1=gg[:, c, 0:16], op=ALU.mult), reads=HK + [("gg", c)], writes=[("ylru", c)])

        def stage_B_smp(resA):
            xnT = resA["xnT"]
            kTs = resA["kTs"]
            kc_f = ov([128, 8, 512])
            vc_f = ov([128, 8, 512])
            KTc = ov([128, 8, 4, 128], BF16)
            vcs = ov([128, 8, VW], BF16)
            vN = ov([128, 4, VW], BF16)
            Es = ov([128, 288], BF16)
            otmp = ov([128, 128])
            rz = ov([128, 128])
            S.dve(lambda e: e.memset(vcs, 1.0), writes=["vcs"])
            S.dve(lambda e: e.memset(vN, 1.0), writes=["vN"])
            first = [True]
            for q in range(4):
                for kc in range(8):
                    S.pe(lambda e, kc=kc, q=q: e.matmul(bank(7)[0:4, :], lhsT=xnT[:, kc, q * 4:(q + 1) * 4], rhs=w_in_bf[:, kc, 2048:2560], start=(kc == 0), stop=(kc == 7)),
                         reads=[("xnT", 0)] + W_IN_K, writes=pk(7))
                S.act(lambda e, q=q: e.activation(out=vN[0:4, q, :].rearrange("p (h d) -> p h d", d=65)[:, :, 0:64],
                                                  in_=bank(7)[0:4, :].rearrange("p (h d) -> p h d", d=64), func=AF.Copy),
                      reads=pk(7) + ["vN"], writes=["vN"])
                for b in range(4):
                    ld(kc_f[:, b, :], ck[q, 1536 + 128 * b:1536 + 128 * (b + 1), :], [("kc_f", b)])
                    ld(vc_f[:, b, :], cv[q, 1536 + 128 * b:1536 + 128 * (b + 1), :], [("vc_f", b)], eng="pool")
                    ld(kc_f[:, 4 + b, :], ck[q, b:2048:16, :], [("kc_f", 4 + b)])
                    ld(vc_f[:, 4 + b, :], cv[q, b:2048:16, :], [("vc_f", 4 + b)], eng="pool")
                for blk in range(8):
                    tb = 2 + blk % 2
                    for c in range(4):
                        S.pe(lambda e, blk=blk, c=c, tb=tb: e.transpose(bank(tb)[:, c * 128:(c + 1) * 128], kc_f[:, blk, c * 128:(c + 1) * 128], ident_f[:]),
                             reads=[("kc_f", blk), "ident_f"], writes=pk(tb))
                    S.act(lambda e, blk=blk, tb=tb: e.activation(out=KTc[:, blk, :, :], in_=bank(tb).rearrange("p (c k) -> p c k", k=128), func=AF.Copy),
                          reads=pk(tb), writes=[("KTc", blk)])
                    S.dve(lambda e, blk=blk: e.tensor_copy(out=vcs[:, blk, :].rearrange("p (h d) -> p h d", d=65)[:, :, 0:64],
                                                           in_=vc_f[:, blk, :].rearrange("p (h d) -> p h d", d=64)),
                          reads=[("vc_f", blk), "vcs"], writes=[("vcs", blk)])
                KSUB = int(os.environ.get("KSUB", "9"))
                if KSUB <= 1:
                    break
                for blk in range(8):
                    for h in range(NH):
                        c, po, par = h // 2, (h % 2) * 64, h % 2
                        col = (blk * 4 + h // 2) * 4
                        S.pe(lambda e, blk=blk, c=c, po=po, col=col, q=q, par=par: e.matmul(bank(par)[:, col:col + 4], lhsT=KTc[po:po + 64, blk, c, :], rhs=qT[po:po + 64, c, q * 4:(q + 1) * 4],
                                                                                           start=True, stop=True),
                             reads=[("KTc", blk), ("qT", c)], writes=pk(par))
                for h in range(NH):
                    c, po, par = h // 2, (h % 2) * 64, h % 2
                    col = (32 + h // 2) * 4
                    S.pe(lambda e, c=c, po=po, col=col, q=q, par=par: e.matmul(bank(par)[0:4, col:col + 4], lhsT=kTs[po:po + 64, c, q * 4:(q + 1) * 4], rhs=qT[po:po + 64, c, q * 4:(q + 1) * 4],
                                                                              start=True, stop=True),
                         reads=[("kTs", c), ("qT", c)], writes=pk(par))
                for par in range(2):
                    S.act(lambda e, par=par: e.activation(out=Es[:, par * 144:(par + 1) * 144], in_=bank(par)[:, 0:144], func=AF.Exp, scale=0.125), reads=pk(par), writes=[("Es", par)])
                    for blk in range(9):
                        np_ = 4 if blk == 8 else 128
                        o0 = par * 144 + blk * 16
                        S.dve(lambda e, blk=blk, np_=np_, o0=o0: e.tensor_tensor(out=Es[0:np_, o0:o0 + 16].rearrange("p (h q) -> p h q", q=4),
                                                                                 in0=Es[0:np_, o0:o0 + 16].rearrange("p (h q) -> p h q", q=4),
                                                                                 in1=msmp[0:np_, blk * 4:(blk + 1) * 4].unsqueeze(1).to_broadcast([np_, 4, 4]), op=ALU.mult),
                              reads=[("Es", par), "msmp"], writes=[("Es", par)])
                if KSUB <= 2:
                    break
                for h in range(NH):
                    hsl = slice(h * 65, (h + 1) * 65)
                    oc = (q * 8 + h) * 4
                    par = h % 2
                    for blk in range(9):
                        col = par * 144 + (blk * 4 + h // 2) * 4
                        if blk < 8:
                            S.pe(lambda e, blk=blk, col=col, oc=oc, hsl=hsl, st_=first[0]: e.matmul(bank(4)[0:65, oc:oc + 4], lhsT=vcs[:, blk, hsl], rhs=Es[:, col:col + 4],
                                                                                                 start=st_, stop=False, skip_group_check=True),
                                 reads=[("vcs", blk), ("Es", par)], writes=pk(4))
                            first[0] = False
                        else:
                            S.pe(lambda e, col=col, oc=oc, hsl=hsl, q=q: e.matmul(bank(4)[0:65, oc:oc + 4], lhsT=vN[0:4, q, hsl], rhs=Es[0:4, col:col + 4],
                                                                                 start=False, stop=True, skip_group_check=True),
                                 reads=["vN", ("Es", par)], writes=pk(4))
            if KSUB <= 3:
                return
            S.dve(lambda e: e.reciprocal(out=rz[64:65, 0:128], in_=bank(4)[64:65, 0:128]), reads=pk(4), writes=["rz"])
            S.dve(lambda e: e.tensor_copy(out=otmp[0:64, 0:128], in_=bank(4)[0:64, 0:128]), reads=pk(4), writes=["otmp"])
            S.pe(lambda e: e.matmul(bank(6)[0:64, 0:128], lhsT=ones_f[64:65, 0:64], rhs=rz[64:65, 0:128], start=True, stop=True), reads=["rz", "ones_f"], writes=pk(6))
            for h in range(NH):
                c, po = h // 2, (h % 2) * 64
                S.dve(lambda e, h=h, c=c, po=po: e.tensor_tensor(out=yatt[po:po + 64, c, 0:16].rearrange("p (s q) -> p s q", q=4),
                                                                 in0=otmp[0:64, 0:128].rearrange("p (s h q) -> p s h q", h=8, q=4)[:, :, h, :],
                                                                 in1=bank(6)[0:64, 0:128].rearrange("p (s h q) -> p s h q", h=8, q=4)[:, :, h, :], op=ALU.mult),
                      reads=["otmp"] + pk(6), writes=[("yatt", h)])

        import os
        LIM = os.environ.get("KLIM", "all")
        nh = {"smpA": 0, "smpL": 0, "smpB": 0, "smp": 0, "setup": 0, "halo1": 5, "A": 0, "L": 0, "B": 0, "C": 0, "own2": 0, "own5": 0}.get(LIM, 8)
        hl = list(range(8 - nh, 8))
        gi = [0]

        def halo_A(s):
            mode = "kv" if s >= 4 else "lru"
            return stage_A(xs, s * SUB, SUB, mode, scr_col=(s - 4) * SUB if s >= 4 else None, ub=s % 2, split=True)

        if hl:
            halo_A(hl[0])()
        cgen = ffn_convert()
        for i, s in enumerate(hl):
            p2 = halo_A(hl[i + 1]) if i + 1 < len(hl) else None
            for yi, _ in enumerate(L2_gen(False, ub=s % 2)):
                if yi % 3 == 0:
                    next(cgen, None)
            if p2 is not None:
                p2()
        for _ in cgen:
            pass
        S.barrier()
        S.dve(lambda e: e.tensor_scalar(out=hstate[:], in0=hstate[:], scalar1=flag_t[:, 0:1], scalar2=None, op0=ALU.mult), reads=["hstate", "flag_t"], writes=["hstate"])
        nown = {"smpA": 0, "smpL": 0, "smpB": 0, "smp": 0, "setup": 0, "halo1": 0, "A": 1, "L": 1, "B": 1, "C": 1, "own1": 1, "own2": 2, "own5": 5, "h8o1": 1}.get(LIM, NSUB)
        for s in range(nown):
            stage_A(xs, HALO + s * SUB, SUB, "own", scr_col=KVH + s * SUB, win_row=(s - 4) * SUB if s >= 4 else None, ub=s % 2)
            if LIM == "A":
                break
            S.barrier()
            if LIM == "L":
                break
            l2 = L2_gen(True, last=(s == NSUB - 1), ub=s % 2)
            stage_B(s, hook=lambda: next(l2, None))
            for _ in l2:
                pass
            S.barrier()
            if DBG and s == 0:
                ld(d_yatt[:, :], yatt[:].rearrange("p c t -> p (c t)"), ["d_yatt"], r=[("yatt", h) for h in range(8)], eng="pool")
                ld(d_ylru[:, :], ylru[:].rearrange("p c t -> p (c t)"), ["d_ylru"], r=[("ylru", c) for c in range(4)], eng="pool")
                S.barrier()
            if LIM == "B":
                break
            stage_C(xs, HALO + s * SUB, SUB, o_y, s * SUB)
            S.barrier()

        if LIM in ("all", "smp", "smpA", "smpL", "smpB"):
            def smp_stores():
                ovl_reset()
                cst = ov([128, 4, 12])
                lst = ov([128, 4, 4])
                o1 = ov([128, 512]); o2 = ov([128, 512]); o3 = ov([128, 512]); o4 = ov([128, 512])
                ue = u_ext2[LAST_UB[0]]
                for c in range(4):
                    S.dve(lambda e, c=c: e.tensor_copy(out=cst[:, c, :].rearrange("p (q j) -> p q j", j=3), in_=u_s[:, c, :, 4:7]), reads=[("u_s", c)], writes=[("cst", c)])
                    S.dve(lambda e, c=c: e.tensor_copy(out=lst[:, c, :], in_=hs_s[:, c, 3:16:4]), reads=[("hs", c, q) for q in range(4)], writes=[("lst", c)])
                for c in range(4):
                    cs = slice(c * 128, (c + 1) * 128)
                    S.pe(lambda e, c=c, cs=cs: e.transpose(bank(0)[0:12, cs], cst[:, c, :], ident_f[:]), reads=[("cst", c), "ident_f"], writes=pk(0))
                    S.pe(lambda e, c=c, cs=cs: e.transpose(bank(1)[0:4, cs], lst[:, c, :], ident_f[:]), reads=[("lst", c), "ident_f"], writes=pk(1))
                    S.pe(lambda e, c=c, cs=cs: e.transpose(bank(2)[0:3, cs], ue[:, c, SUB:SUB + 3], ident_f[:]), reads=[("u_ext", LAST_UB[0], c), "ident_f"], writes=pk(2))
                    S.pe(lambda e, c=c, cs=cs: e.transpose(bank(3)[0:1, cs], hstate[:, c:c + 1], ident_f[:]), reads=["hstate", "ident_f"], writes=pk(3))
                S.act(lambda e: e.activation(out=o1[0:12], in_=bank(0)[0:12, :], func=AF.Copy), reads=pk(0), writes=["o1"])
                S.dve(lambda e: e.tensor_copy(out=o2[0:4], in_=bank(1)[0:4, :]), reads=pk(1), writes=["o2"])
                S.act(lambda e: e.activation(out=o3[0:3], in_=bank(2)[0:3, :], func=AF.Copy), reads=pk(2), writes=["o3"])
                S.dve(lambda e: e.tensor_copy(out=o4[0:1], in_=bank(3)[0:1, :]), reads=pk(3), writes=["o4"])
                ld(o_convs[:, :], o1[0:12], ["o_convs"], r=["o1"], eng="pool")
                ld(o_lrus[:, :], o2[0:4], ["o_lrus"], r=["o2"], eng="sp")
                ld(o_convp[:, :], o3[0:3], ["o_convp"], r=["o3"], eng="pool")
                ld(o_lrup[:, :], o4[0:1], ["o_lrup"], r=["o4"], eng="sp")

            resA = stage_A(x_smp, 0, 16, "own", win_row=0, smp=True)
            if LIM != "smpA":
                stage_L_smp()
                S.barrier()
                if LIM != "smpL":
                    stage_B_smp(resA)
                    S.barrier()
                    if LIM != "smpB":
                        stage_C(x_smp, 0, 16, o_ys, 0)
                        S.barrier()
            smp_stores()
        S.emit(nc, st)
    return nc


_NC_CACHE = {}


def kernel(**inputs):
    f32 = np.float32
    inp = {k: np.asarray(v) for k, v in inputs.items()}
    xp = inp["x_prompt"].astype(f32, copy=False)
    consts = _consts()
    if "nc" not in _NC_CACHE:
        _NC_CACHE["nc"] = build_nc()
    nc = _NC_CACHE["nc"]
    in_maps = []
    for c in range(8):
        b, half = c // 2, c % 2
        xs = np.zeros((HALO + TOK, D), f32)
        if half == 1:
            xs[:] = xp[b]
        else:
            xs[HALO:] = xp[b, :TOK]
        valid = np.ones((KVH + TOK,), f32)
        if half == 0:
            valid[:KVH] = 0.0
        m = {
            "xs": xs,
            "flag": np.full((128, 1), float(half), f32),
            "validc": np.ascontiguousarray(valid.reshape(48, 128).T),
            "x_smp": np.ascontiguousarray(inp["x_sample"][4 * c:4 * c + 4].reshape(16, D)),
            "sconv": np.ascontiguousarray(inp["state_conv"][0, 4 * c:4 * c + 4].reshape(12, DL)),
            "slru": np.ascontiguousarray(inp["state_lru"][0, 4 * c:4 * c + 4]),
            "ck": np.ascontiguousarray(inp["cache_k"][0, 4 * c:4 * c + 4].reshape(4, 2048, 512)),
            "cv": np.ascontiguousarray(inp["cache_v"][0, 4 * c:4 * c + 4].reshape(4, 2048, 512)),
            "norm_mix": inp["norm_mix"][0], "w_in": inp["w_in"][0], "conv_w": inp["conv_w"][0], "conv_b": inp["conv_b"][0],
            "lru_w_a": inp["lru_w_a"][0], "lru_b_a": inp["lru_b_a"][0], "lru_w_x": inp["lru_w_x"][0], "lru_b_x": inp["lru_b_x"][0],
            "lru_lambda": inp["lru_lambda"][0], "out_norm_lru": inp["out_norm_lru"][0], "out_norm_attn": inp["out_norm_attn"][0],
            "w_out": inp["w_out"][0], "norm_ffn": inp["norm_ffn"][0], "w_gate": inp["w_gate"][0], "w_up": inp["w_up"][0],
            "w_down": inp["w_down"][0], "norm_final": inp["norm_final"].reshape(1, D),
            "c_ident": consts["ident"], "c_m1": consts["m1"], "c_m4": consts["m4"], "c_m16": consts["m16"], "c_mxa": consts["mxa"], "c_mxf": consts["mxf"], "c_msmp": consts["msmp"],
        }
        in_maps.append({k: np.ascontiguousarray(v) for k, v in m.items()})
    res = run_bass_kernel_spmd(nc, in_maps, core_ids=list(range(8)))
    R = res.results
    _NC_CACHE['last'] = R
    y_prompt = np.zeros((4, 8192, D), f32)
    conv_p = np.zeros((1, 4, 3, DL), f32)
    lru_p = np.zeros((1, 4, DL), f32)
    kw = np.zeros((1, 4, 2048, NH, DH), f32)
    vw = np.zeros((1, 4, 2048, NH, DH), f32)
    y_s = np.zeros((32, 4, D), f32)
    conv_s = np.zeros((1, 32, 3, DL), f32)
    lru_s = np.zeros((1, 32, DL), f32)
    kn = np.zeros((1, 32, 4, NH, DH), f32)
    vn = np.zeros((1, 32, 4, NH, DH), f32)
    for c in range(8):
        b, half = c // 2, c % 2
        r = R[c]
        y_prompt[b, half * TOK:(half + 1) * TOK] = r["o_y"]
        if half == 1:
            conv_p[0, b] = r["o_convp"]
            lru_p[0, b] = r["o_lrup"][0]
            kw[0, b] = r["o_kwin"].reshape(2048, NH, DH)
            vw[0, b] = r["o_vwin"].reshape(2048, NH, DH)
        y_s[4 * c:4 * c + 4] = r["o_ys"].reshape(4, 4, D)
        conv_s[0, 4 * c:4 * c + 4] = r["o_convs"].reshape(4, 3, DL)
        lru_s[0, 4 * c:4 * c + 4] = r["o_lrus"]
        kn[0, 4 * c:4 * c + 4] = r["o_knew"].reshape(4, 4, NH, DH)
        vn[0, 4 * c:4 * c + 4] = r["o_vnew"].reshape(4, 4, NH, DH)
    return (y_prompt, y_s, conv_p, lru_p, kw, vw, conv_s, lru_s, kn, vn)
```

```python
from contextlib import ExitStack
import numpy as np
import ml_dtypes
import concourse.bass as bass
import concourse.mybir as mybir
from concourse.bass_utils import run_bass_kernel_spmd

F32 = mybir.dt.float32
BF16 = mybir.dt.bfloat16
AF = mybir.ActivationFunctionType
ALU = mybir.AluOpType

import os as _os
SAME_ENGINE_SYNC = _os.environ.get("KSES", "1") == "1"
RAW_ONLY = _os.environ.get("KRAW", "0") == "1"

D = 1024
DL = 512
NH = 8
DH = 64
DFF = 2816
NFF = 22
DIN = 2560
TOK = 4096
SUB = 512
NSUB = 8
HALO = 4096
KVH = 2048
EPS = 1e-6
VW = NH * 65


class Op:
    __slots__ = ("eng", "fn", "deps", "raw", "needs_inc", "kind", "sem", "val", "prev_slot_op", "name", "nobar")

    def __init__(self, eng, fn, kind, name=""):
        self.eng = eng
        self.fn = fn
        self.kind = kind
        self.deps = set()
        self.raw = set()
        self.needs_inc = False
        self.sem = None
        self.val = None
        self.prev_slot_op = None
        self.name = name
        self.nobar = False


class Sched:
    ENGS = ["pe", "act", "dve", "pool", "sp"]
    NSLOT = {"sp": 12, "pool": 8}

    def __init__(self):
        self.ops = {e: [] for e in self.ENGS}
        self.last_w = {}
        self.readers = {}
        self.pending = {e: set() for e in self.ENGS}
        self.bar_idx = {e: 0 for e in self.ENGS}

    def add(self, eng, fn, reads=(), writes=(), kind="compute", name="", nobar=False):
        op = Op(eng, fn, kind, name)
        op.nobar = nobar
        psr = [k for k in reads if isinstance(k, tuple) and k[0] == "ps"]
        if psr:
            reads = [k for k in reads if k not in psr]
            writes = list(writes) + [k for k in psr if k not in writes]
        for k in reads:
            w = self.last_w.get(k)
            if w is not None:
                op.deps.add(w)
                op.raw.add(w)
        for k in psr:
            w = self.last_w.get(k)
            if w is not None:
                op.raw.add(w)
        for k in writes:
            w = self.last_w.get(k)
            if w is not None:
                op.deps.add(w)
            lastc = {}
            for r in self.readers.get(k, ()):
                if r.kind == "dma":
                    op.deps.add(r)
                else:
                    lastc[r.eng] = r
            for r in lastc.values():
                op.deps.add(r)
        for k in reads:
            self.readers.setdefault(k, []).append(op)
        for k in writes:
            self.last_w[k] = op
            self.readers[k] = []
        if self.pending[eng]:
            op.deps |= self.pending[eng]
            self.pending[eng] = set()
        op.deps.discard(op)
        self.ops[eng].append(op)
        return op

    def barrier(self):
        lasts = set()
        for e in self.ENGS:
            ops = self.ops[e]
            for op in reversed(ops):
                if op.kind == "compute":
                    lasts.add(op)
                    break
            for op in ops[self.bar_idx[e]:]:
                if op.kind == "dma" and not op.nobar:
                    lasts.add(op)
            self.bar_idx[e] = len(ops)
        for e in self.ENGS:
            self.pending[e] |= lasts

    def pe(self, fn, reads=(), writes=(), **kw):
        return self.add("pe", fn, reads, writes, **kw)

    def act(self, fn, reads=(), writes=(), **kw):
        return self.add("act", fn, reads, writes, **kw)

    def dve(self, fn, reads=(), writes=(), **kw):
        return self.add("dve", fn, reads, writes, **kw)

    def pool(self, fn, reads=(), writes=(), **kw):
        return self.add("pool", fn, reads, writes, **kw)

    def dma(self, fn, reads=(), writes=(), eng="sp", **kw):
        return self.add(eng, fn, reads, writes, kind="dma", **kw)

    def emit(self, nc, stack):
        for e in self.ENGS:
            for op in self.ops[e]:
                keep = set()
                for d in op.deps:
                    if d.eng == op.eng and d.kind == "compute" and op.kind == "compute":
                        if op.eng == "pe" or not SAME_ENGINE_SYNC:
                            continue
                        if RAW_ONLY and d not in op.raw:
                            continue
                    keep.add(d)
                op.deps = keep
                for d in keep:
                    d.needs_inc = True
        sems = {}
        for e in self.ENGS:
            sems[e] = stack.enter_context(nc.semaphore("prog_" + e))
        dma_sems = {}
        for e, n in self.NSLOT.items():
            dma_sems[e] = [stack.enter_context(nc.semaphore("dma_%s_%d" % (e, i))) for i in range(n)]
        for e in self.ENGS:
            cnt = 0
            dcnt = 0
            slot_last = {}
            for op in self.ops[e]:
                if op.kind == "dma":
                    n = self.NSLOT[e]
                    slot = dcnt % n
                    rnd = dcnt // n + 1
                    op.sem = dma_sems[e][slot]
                    op.val = 16 * rnd
                    op.prev_slot_op = slot_last.get(slot)
                    slot_last[slot] = op
                    dcnt += 1
                elif op.needs_inc:
                    cnt += 1
                    op.sem = sems[e]
                    op.val = cnt
        import os
        if os.environ.get("KDEBUG"):
            for e in self.ENGS:
                ops = self.ops[e]
                print(e, "ops", len(ops), "incs", sum(1 for o in ops if o.kind == "compute" and o.needs_inc),
                      "dmas", sum(1 for o in ops if o.kind == "dma"), "maxval", max([o.val or 0 for o in ops] + [0]))
        block = stack.enter_context(nc.Block())
        sched = self

        def run(e, eng):
            waited = {}
            for op in sched.ops[e]:
                need = {}
                dl = list(op.deps)
                if op.prev_slot_op is not None:
                    dl.append(op.prev_slot_op)
                for d in dl:
                    k = d.sem
                    if need.get(k, (None, 0))[1] < d.val:
                        need[k] = (d.sem, d.val)
                for k, (s, v) in need.items():
                    if waited.get(k, 0) < v:
                        eng.wait_ge(s, v)
                        waited[k] = v
                inst = op.fn(eng)
                if op.kind == "dma":
                    inst.then_inc(op.sem, 16)
                elif op.needs_inc:
                    inst.then_inc(op.sem, 1)
            last = {}
            for op in sched.ops[e]:
                if op.kind == "dma":
                    last[op.sem] = (op.sem, op.val)
            for k, (s, v) in last.items():
                if waited.get(k, 0) < v:
                    eng.wait_ge(s, v)
                    waited[k] = v

        @block.tensor
        def _(eng):
            run("pe", eng)

        @block.scalar
        def _(eng):
            run("act", eng)

        @block.vector
        def _(eng):
            run("dve", eng)

        @block.gpsimd
        def _(eng):
            run("pool", eng)

        @block.sync
        def _(eng):
            run("sp", eng)


def _consts():
    c = {}
    c["ident"] = np.eye(128, dtype=np.float32)
    kk = np.arange(128)[:, None]
    qq = np.arange(256)[None, :]
    band = ((qq >= kk) & (qq <= kk + 128)).astype(np.float32)
    m1 = np.concatenate([band[:, 128:256], band, band, band, band[:, 0:128]], axis=1)
    m4 = np.concatenate([np.concatenate([band[:, 128:256], band[:, 0:128]], 1)] * 4, axis=1)
    qi = np.arange(32)[None, :]
    ma = (kk >= qi).astype(np.float32)
    mb = ((kk <= qi) & (kk < 32)).astype(np.float32)
    m16 = np.concatenate([ma] * 16 + [mb] * 16, axis=1)
    ii_ = np.arange(128)[None, :]
    mx = np.zeros((128, 5, 128), np.float32)
    for kb in range(5):
        jp = 128 * (kb - 4) + np.arange(128)[:, None]
        dd = ii_ - jp
        v4 = (dd >= 0) & (dd <= 128)
        v16 = (dd >= 0) & (dd % 4 == 0) & (dd <= 512)
        mx[:, kb, :] = v4.astype(np.float32) + v16.astype(np.float32)
    c["mxa"] = mx[:, 1:5, :].reshape(128, 512).astype(ml_dtypes.bfloat16)
    c["mxf"] = np.concatenate([mx[:, 0, :]] * 4, axis=1).astype(ml_dtypes.bfloat16)
    c["m1"] = m1.astype(ml_dtypes.bfloat16)
    c["m4"] = m4.astype(ml_dtypes.bfloat16)
    c["m16"] = m16.astype(ml_dtypes.bfloat16)
    ms = np.zeros((128, 9, 4), np.float32)
    for i in range(4):
        qpos = 2048 + i
        for b in range(4):
            for k in range(128):
                row = 1536 + 128 * b + k
                dist = qpos - row
                cnt = 0
                if 0 <= dist <= 128:
                    cnt += 1
                if dist % 4 == 0 and 0 <= dist <= 512:
                    cnt += 1
                ms[k, b, i] = cnt
        for k in range(128):
            ms[k, 4 + i, i] = 1.0
        for k in range(4):
            dist = i - k
            if dist >= 0:
                ms[k, 8, i] = 1.0 + (2.0 if dist == 0 else 0.0)
    c["msmp"] = ms.reshape(128, 36).astype(ml_dtypes.bfloat16)
    return c


def build_nc():
    nc = bass.Bass("TRN2", target_bir_lowering=False)
    S = Sched()

    def din(name, shape, dt=F32):
        return nc.dram_tensor(name, list(shape), dt, kind="ExternalInput").ap()

    def dout(name, shape, dt=F32):
        return nc.dram_tensor(name, list(shape), dt, kind="ExternalOutput").ap()

    def dscr(name, shape, dt=BF16):
        return nc.dram_tensor(name, list(shape), dt, kind="Internal").ap()

    xs = din("xs", [HALO + TOK, D])
    flag = din("flag", [128, 1])
    validc = din("validc", [128, 48])
    x_smp = din("x_smp", [16, D])
    sconv = din("sconv", [12, DL])
    slru = din("slru", [4, DL])
    ck = din("ck", [4, 2048, 512])
    cv = din("cv", [4, 2048, 512])
    norm_mix = din("norm_mix", [D])
    w_in = din("w_in", [D, DIN])
    conv_w = din("conv_w", [4, DL])
    conv_b = din("conv_b", [DL])
    lru_w_a = din("lru_w_a", [8, 64, 64])
    lru_b_a = din("lru_b_a", [DL])
    lru_w_x = din("lru_w_x", [8, 64, 64])
    lru_b_x = din("lru_b_x", [DL])
    lru_lambda = din("lru_lambda", [DL])
    out_norm_lru = din("out_norm_lru", [DL])
    out_norm_attn = din("out_norm_attn", [DL])
    w_out = din("w_out", [D, D])
    norm_ffn = din("norm_ffn", [D])
    w_gate = din("w_gate", [D, DFF])
    w_up = din("w_up", [D, DFF])
    w_down = din("w_down", [DFF, D])
    norm_final = din("norm_final", [1, D])
    c_ident = din("c_ident", [128, 128])
    c_m1 = din("c_m1", [128, 1024], BF16)
    c_m4 = din("c_m4", [128, 1024], BF16)
    c_m16 = din("c_m16", [128, 1024], BF16)
    c_mxa = din("c_mxa", [128, 512], BF16)
    c_mxf = din("c_mxf", [128, 512], BF16)
    c_msmp = din("c_msmp", [128, 36], BF16)

    o_y = dout("o_y", [TOK, D])
    o_ys = dout("o_ys", [16, D])
    o_convp = dout("o_convp", [3, DL])
    o_lrup = dout("o_lrup", [1, DL])
    o_kwin = dout("o_kwin", [2048, 512])
    o_vwin = dout("o_vwin", [2048, 512])
    o_convs = dout("o_convs", [12, DL])
    o_lrus = dout("o_lrus", [4, DL])
    o_knew = dout("o_knew", [16, 512])
    o_vnew = dout("o_vnew", [16, 512])
    import os
    DBG = bool(os.environ.get("KDBG"))
    if DBG:
        d_yatt = dout("d_yatt", [128, 2048], BF16)
        d_ylru = dout("d_ylru", [128, 2048], BF16)
        d_E = dout("d_E", [128, 1024], BF16)
        d_O = dout("d_O", [128, 512])

    wg_s = dscr("wg_s", [NFF, 128, 8, 128])
    wu_s = dscr("wu_s", [NFF, 128, 8, 128])
    wd_s = dscr("wd_s", [NFF, 128, D])
    Kscr = dscr("Kscr", [512, KVH + TOK])
    Vscr = dscr("Vscr", [KVH + TOK, VW])

    st = ExitStack()
    with st:
        def sb(name, shape, dt=F32):
            return st.enter_context(nc.sbuf_tensor(name, list(shape), dt))

        psum = st.enter_context(nc.psum_tensor("psum", [128, 4096], F32))

        def bank(b, n=1):
            return psum[:, b * 512:(b + n) * 512]

        def bankb(b):
            return psum[:, b * 512:(b + 1) * 512].bitcast(BF16)

        def pk(b, n=1):
            return [("ps", b + i) for i in range(n)]

        w_in_bf = sb("w_in_bf", [128, 8, DIN], BF16)
        w_out_bf = sb("w_out_bf", [128, 8, D], BF16)
        ident_f = sb("ident_f", [128, 128], F32)
        ident_b = sb("ident_b", [128, 128], BF16)
        ones_f = sb("ones_f", [128, 64], F32)
        ones_b = sb("ones_b", [128, 2], BF16)
        eps_t = sb("eps_t", [128, 1], F32)
        one_t = sb("one_t", [128, 1], F32)
        m1 = sb("m1", [128, 1024], BF16)
        mxa = sb("mxa", [128, 512], BF16)
        mxf = sb("mxf", [128, 512], BF16)
        msmp = sb("msmp", [128, 36], BF16)
        g_mix = sb("g_mix", [128, 8])
        g_ffn = sb("g_ffn", [128, 8])
        g_ol = sb("g_ol", [128, 4])
        g_oa = sb("g_oa", [128, 4])
        cw = sb("cw", [128, 4, 4])
        cb = sb("cb", [128, 4])
        b_a = sb("b_a", [128, 4])
        b_x = sb("b_x", [128, 4])
        lam = sb("lam", [128, 4])
        cvec = sb("cvec", [128, 4])
        gfin = sb("gfin", [128, D])
        flag_t = sb("flag_t", [128, 1])
        valid_t = sb("valid_t", [128, 48])
        Wa_bd = sb("Wa_bd", [128, 4, 128], BF16)
        Wx_bd = sb("Wx_bd", [128, 4, 128], BF16)
        hstate = sb("hstate", [128, 4])
        gg = sb("gg", [128, 4, SUB], BF16)
        qT = sb("qT", [128, 4, SUB], BF16)
        ylru = sb("ylru", [128, 4, SUB], BF16)
        yatt = sb("yatt", [128, 4, SUB], BF16)
        u_ext2 = [sb("u_extA", [128, 4, SUB + 3]), sb("u_extB", [128, 4, SUB + 3])]
        ssq = sb("ssq", [128, 8])
        rstd = sb("rstd", [128, 8])
        u_s = sb("u_s", [128, 4, 4, 7])
        h0_s = sb("h0_s", [128, 4, 4])
        hs_s = sb("hs_s", [128, 4, 16])

        OVL = 100 * 1024 // 2
        ovl = sb("ovl", [128, OVL], BF16)
        ovl_pos = [0]

        def ovl_reset():
            ovl_pos[0] = 0

        def ov(shape, dt=F32):
            n = int(np.prod(shape[1:]))
            nb = n * (4 if dt == F32 else 2)
            nb = (nb + 63) // 64 * 64
            a = ovl_pos[0]
            assert a + nb // 2 <= OVL, ("overlay overflow", a, nb)
            ovl_pos[0] = a + nb // 2
            v = ovl[:, a:a + (n * (2 if dt == F32 else 1))]
            if dt == F32:
                v = v.bitcast(F32)
            if len(shape) == 3:
                v = v.rearrange("p (a b) -> p a b", b=shape[2])
            elif len(shape) == 4:
                v = v.rearrange("p (a b c) -> p a b c", b=shape[2], c=shape[3])
            return v[0:shape[0]]

        def ld(out, in_, w, r=(), eng="sp", slow=False, nobar=False):
            if slow:
                S.dma(lambda e: e.dma_start(out=out, in_=in_, allow_slow_non_contiguous=True), reads=r, writes=w, eng=eng, nobar=nobar)
            else:
                S.dma(lambda e: e.dma_start(out=out, in_=in_), reads=r, writes=w, eng=eng, nobar=nobar)

        ld(ident_f[:], c_ident[:, :], ["ident_f"])
        ld(m1[:], c_m1[:, :], ["m1"])
        ld(mxa[:], c_mxa[:, :], ["mxa"])
        ld(mxf[:], c_mxf[:, :], ["mxf"])
        ld(msmp[:], c_msmp[:, :], ["msmp"])
        ld(g_mix[:], norm_mix.rearrange("(c p) -> p c", p=128), ["g_mix"], slow=True)
        ld(g_ffn[:], norm_ffn.rearrange("(c p) -> p c", p=128), ["g_ffn"], slow=True)
        ld(g_ol[:], out_norm_lru.rearrange("(c p) -> p c", p=128), ["g_ol"], slow=True)
        ld(g_oa[:], out_norm_attn.rearrange("(c p) -> p c", p=128), ["g_oa"], slow=True)
        for j in range(4):
            ld(cw[:, :, j], conv_w[j].rearrange("(c p) -> p c", p=128), [("cw", j)], slow=True)
        ld(cb[:], conv_b.rearrange("(c p) -> p c", p=128), ["cb"], slow=True)
        ld(b_a[:], lru_b_a.rearrange("(c p) -> p c", p=128), ["b_a"], slow=True)
        ld(b_x[:], lru_b_x.rearrange("(c p) -> p c", p=128), ["b_x"], slow=True)
        ld(lam[:], lru_lambda.rearrange("(c p) -> p c", p=128), ["lam"], slow=True)
        ld(gfin[:], norm_final[0:1, :].partition_broadcast(128), ["gfin"])
        ld(flag_t[:], flag[:, :], ["flag_t"])
        ld(valid_t[:], validc[:, :], ["valid_t"])
        CWK = [("cw", j) for j in range(4)]
        for c in range(4):
            for q in range(4):
                ld(u_s[:, c, q, 0:3], sconv[q * 3:(q + 1) * 3, c * 128:(c + 1) * 128].rearrange("j p -> p j"), [("u_s0", c, q)], slow=True, eng="pool")
            ld(h0_s[:, c, :], slru[:, c * 128:(c + 1) * 128].rearrange("s p -> p s"), [("h0_s", c)], slow=True, eng="pool")

        S.dve(lambda e: e.memset(ones_f[:], 1.0), writes=["ones_f"])
        S.dve(lambda e: e.memset(ones_b[:], 1.0), writes=["ones_b"])
        S.dve(lambda e: e.memset(eps_t[:], EPS), writes=["eps_t"])
        S.dve(lambda e: e.memset(one_t[:], 1.0), writes=["one_t"])
        S.dve(lambda e: e.memset(hstate[:], 0.0), writes=["hstate"])
        S.dve(lambda e: e.memset(u_ext2[0][:, :, 0:3], 0.0), writes=[("u_carry", 0)])
        S.dve(lambda e: e.tensor_copy(out=ident_b[:], in_=ident_f[:]), reads=["ident_f"], writes=["ident_b"])
        S.act(lambda e: e.activation(out=cvec[:], in_=lam[:], func=AF.Exp, scale=-1.0), reads=["lam"], writes=["cvec"])
        S.act(lambda e: e.activation(out=cvec[:], in_=cvec[:], func=AF.Ln, bias=one_t[:, 0:1], scale=1.0), reads=["cvec", "one_t"], writes=["cvec"])
        S.dve(lambda e: e.tensor_scalar(out=cvec[:], in0=cvec[:], scalar1=-8.0, scalar2=None, op0=ALU.mult), reads=["cvec"], writes=["cvec"])

        ovl_reset()
        wst = [ov([128, DIN]), ov([128, DIN])]
        for kc in range(8):
            b = kc % 2
            ld(wst[b], w_in[kc * 128:(kc + 1) * 128, :], [("wst", b)])
            if kc % 2 == 0:
                S.dve(lambda e, kc=kc, b=b: e.tensor_scalar(out=w_in_bf[:, kc, :], in0=wst[b], scalar1=g_mix[:, kc:kc + 1], scalar2=None, op0=ALU.mult),
                      reads=[("wst", b), "g_mix"], writes=[("w_in_bf", kc)])
            else:
                S.act(lambda e, kc=kc, b=b: e.activation(out=w_in_bf[:, kc, :], in_=wst[b], func=AF.Copy, scale=g_mix[:, kc:kc + 1]),
                      reads=[("wst", b), "g_mix"], writes=[("w_in_bf", kc)])
        W_IN_K = [("w_in_bf", kc) for kc in range(8)]
        for kc in range(8):
            b = kc % 2
            ld(wst[b][:, 0:D], w_out[kc * 128:(kc + 1) * 128, :], [("wst", b)])
            gsrc = g_ol[:, kc:kc + 1] if kc < 4 else g_oa[:, kc - 4:kc - 3]
            if kc % 2 == 0:
                S.dve(lambda e, kc=kc, b=b, gsrc=gsrc: e.tensor_scalar(out=w_out_bf[:, kc, :], in0=wst[b][:, 0:D], scalar1=gsrc, scalar2=None, op0=ALU.mult),
                      reads=[("wst", b), "g_ol", "g_oa"], writes=[("w_out_bf", kc)])
            else:
                S.act(lambda e, kc=kc, b=b, gsrc=gsrc: e.activation(out=w_out_bf[:, kc, :], in_=wst[b][:, 0:D], func=AF.Copy, scale=gsrc),
                      reads=[("wst", b), "g_ol", "g_oa"], writes=[("w_out_bf", kc)])
        W_OUT_K = [("w_out_bf", kc) for kc in range(8)]
        bdst = ov([128, 4, 128])
        for (wsrc, wdst, nm) in ((lru_w_a, Wa_bd, "Wa_bd"), (lru_w_x, Wx_bd, "Wx_bd")):
            S.dve(lambda e: e.memset(bdst, 0.0), writes=["bdst"])
            for c in range(4):
                ld(bdst[0:64, c, 0:64], wsrc[2 * c], ["bdst"])
                ld(bdst[64:128, c, 64:128], wsrc[2 * c + 1], ["bdst"])
            S.dve(lambda e, wdst=wdst: e.tensor_copy(out=wdst[:], in_=bdst), reads=["bdst"], writes=[nm])
        S.barrier()

        def ov_top(off_bytes, shape, dt=F32):
            n = int(np.prod(shape[1:]))
            a0 = OVL - off_bytes // 2
            v = ovl[:, a0:a0 + (n * (2 if dt == F32 else 1))]
            if dt == F32:
                v = v.bitcast(F32)
            return v[0:shape[0]]

        HW = DFF // 2
        CVB = (OVL * 2 - 36 * 1024) if os.environ.get("KCV", "mid") == "mid" else 16896
        cv_f = [ov_top(CVB, [128, HW]), ov_top(CVB - 5632, [128, HW]), ov_top(CVB - 11264, [128, HW])]
        cv_b = [ov_top(CVB - 16896, [128, HW], BF16), ov_top(CVB - 16896 - 2816, [128, HW], BF16)]
        L_OFF = 58 * 1024 // 2

        def ffn_convert():
            pieces = []
            for (wsrc, wdst) in ((w_gate, wg_s), (w_up, wu_s)):
                for kc in range(8):
                    for hf in range(2):
                        pieces.append(("gu", wsrc, wdst, kc, hf))
            for f in range(NFF):
                pieces.append(("d", f))

            def load(i):
                p = pieces[i]
                b = i % 3
                if p[0] == "gu":
                    _, wsrc, wdst, kc, hf = p
                    ld(cv_f[b], wsrc[kc * 128:(kc + 1) * 128, hf * HW:(hf + 1) * HW], [("cv_f", b)])
                else:
                    f = p[1]
                    ld(cv_f[b][:, 0:D], w_down[f * 128:(f + 1) * 128, :], [("cv_f", b)])

            def cast_store(i):
                p = pieces[i]
                fb = i % 3
                b = i % 2
                if p[0] == "gu":
                    _, wsrc, wdst, kc, hf = p
                    if b == 0:
                        S.dve(lambda e: e.tensor_scalar(out=cv_b[b], in0=cv_f[fb], scalar1=g_ffn[:, kc:kc + 1], scalar2=None, op0=ALU.mult),
                              reads=[("cv_f", fb), "g_ffn"], writes=[("cv_b", b)])
                    else:
                        S.act(lambda e: e.activation(out=cv_b[b], in_=cv_f[fb], func=AF.Copy, scale=g_ffn[:, kc:kc + 1]),
                              reads=[("cv_f", fb), "g_ffn"], writes=[("cv_b", b)])
                    ld(wdst[hf * 11:(hf + 1) * 11, :, kc, :].rearrange("f p n -> p f n"), cv_b[b].rearrange("p (f n) -> p f n", n=128),
                       ["ffn_scr"], r=[("cv_b", b)], eng="pool")
                else:
                    f = p[1]
                    if b == 0:
                        S.dve(lambda e: e.tensor_copy(out=cv_b[b][:, 0:D], in_=cv_f[fb][:, 0:D]), reads=[("cv_f", fb)], writes=[("cv_b", b)])
                    else:
                        S.act(lambda e: e.activation(out=cv_b[b][:, 0:D], in_=cv_f[fb][:, 0:D], func=AF.Copy), reads=[("cv_f", fb)], writes=[("cv_b", b)])
                    ld(wd_s[f], cv_b[b][:, 0:D], ["ffn_scr"], r=[("cv_b", b)], eng="pool")

            load(0)
            load(1)
            yield
            for i in range(len(pieces)):
                if i + 2 < len(pieces):
                    load(i + 2)
                cast_store(i)
                yield

        A_END = [0]
        LAST_UB = [0]

        def stage_A(src, row0, NT, mode, scr_col=None, win_row=None, smp=False, ub=0, split=False):
            ovl_reset()
            bs = min(128, NT)
            nb = NT // bs
            xblk = [ov([128, D]), ov([128, D])]
            xnb = [ov([128, D], BF16), ov([128, D], BF16)]
            xnT = ov([128, 8, SUB], BF16)
            junk = ov([128, D], BF16)
            kTs = ov([128, 4, SUB], BF16)
            vst = ov([128, 4, VW], BF16)
            f32st = [ov([128, 512]), ov([128, 512])]
            A_END[0] = ovl_pos[0]
            fcount = [0]
            pc1 = [0]

            def tm_proj1(blk, col0, evac):
                pb = pc1[0] % 4
                pc1[0] += 1
                for kc in range(8):
                    S.pe(lambda e, kc=kc, pb=pb: e.matmul(bank(pb)[0:bs, :], lhsT=xnT[:, kc, blk * bs:(blk + 1) * bs], rhs=w_in_bf[:, kc, col0:col0 + 512],
                                                          start=(kc == 0), stop=(kc == 7)),
                         reads=[("xnT", blk)] + W_IN_K, writes=pk(pb))
                evac(pb)

            def tm_block(blk):
                if mode not in ("own", "kv"):
                    return

                def ev(pb):
                    S.act(lambda e: e.activation(out=vst[0:bs, blk, :].rearrange("p (h d) -> p h d", d=65)[:, :, 0:64],
                                                 in_=bank(pb)[0:bs, :].rearrange("p (h d) -> p h d", d=64), func=AF.Copy),
                          reads=pk(pb), writes=[("vst", blk)])
                    if smp:
                        S.dve(lambda e: e.memset(vst[0:bs, blk, :].rearrange("p (h d) -> p h d", d=65)[:, :, 64:65], 1.0), writes=[("vst", blk, "v")])
                    else:
                        vb = scr_col // 128 + blk
                        S.dve(lambda e: e.tensor_copy(out=vst[:, blk, :].rearrange("p (h d) -> p h d", d=65)[:, :, 64:65],
                                                      in_=valid_t[:, vb:vb + 1].unsqueeze(1).to_broadcast([128, 8, 1])),
                              reads=["valid_t"], writes=[("vst", blk, "v")])
                    if win_row is not None:
                        fb = fcount[0] % 2
                        fcount[0] += 1
                        S.dve(lambda e: e.tensor_copy(out=f32st[fb][0:bs], in_=bank(pb)[0:bs, :]), reads=pk(pb), writes=[("f32st", fb)])
                        dst = (o_vnew if smp else o_vwin)
                        ld(dst[win_row + blk * bs: win_row + (blk + 1) * bs, :], f32st[fb][0:bs], ["o_v"], r=[("f32st", fb)], eng="pool")
                tm_proj1(blk, 2048, ev)
                if win_row is not None:
                    def ev2(pb):
                        fb = fcount[0] % 2
                        fcount[0] += 1
                        S.dve(lambda e: e.tensor_copy(out=f32st[fb][0:bs], in_=bank(pb)[0:bs, :]), reads=pk(pb), writes=[("f32st", fb)])
                        dst = (o_knew if smp else o_kwin)
                        ld(dst[win_row + blk * bs: win_row + (blk + 1) * bs, :], f32st[fb][0:bs], ["o_k"], r=[("f32st", fb)], eng="pool")
                    tm_proj1(blk, 1536, ev2)
                if blk == nb - 1 and not smp:
                    ld(Vscr[scr_col:scr_col + SUB, :].rearrange("(b p) c -> p b c", p=128), vst, ["Vscr"],
                       r=[("vst", k_) for k_ in range(4)] + [("vst", k_, "v") for k_ in range(4)], eng="pool")

            for blk in range(nb):
                b = blk % 2
                ld(xblk[b][0:bs], src[row0 + blk * bs: row0 + (blk + 1) * bs, :], [("xblk", b)])
                S.dve(lambda e: e.memset(ssq[0:bs, 0:1], 0.0), writes=["ssq"])
                S.act(lambda e, b=b, blk=blk: e.activation(out=junk[0:bs], in_=xblk[b][0:bs], func=AF.Square, accum_out=ssq[0:bs, 0:1]),
                      reads=[("xblk", b)], writes=["junk", "ssq"])
                S.act(lambda e: e.activation(out=ssq[0:bs, 0:1], in_=ssq[0:bs, 0:1], func=AF.Sqrt, scale=1.0 / D, bias=eps_t[0:bs, 0:1]),
                      reads=["ssq", "eps_t"], writes=["ssq"])
                S.dve(lambda e: e.reciprocal(out=rstd[0:bs, 0:1], in_=ssq[0:bs, 0:1]), reads=["ssq"], writes=["rstd"])
                S.dve(lambda e, b=b: e.tensor_scalar(out=xnb[b][0:bs], in0=xblk[b][0:bs], scalar1=rstd[0:bs, 0:1], scalar2=None, op0=ALU.mult),
                      reads=[("xblk", b), "rstd"], writes=[("xnb", b)])
                pb = 6 + b
                pv = bankb(pb)
                for kc in range(8):
                    S.pe(lambda e, kc=kc, b=b, pv=pv: e.transpose(pv[:, kc * 128: kc * 128 + bs], xnb[b][0:bs, kc * 128:(kc + 1) * 128], ident_b[0:bs, 0:bs]),
                         reads=[("xnb", b), "ident_b"], writes=pk(pb))
                S.dve(lambda e, pv=pv, blk=blk: e.tensor_copy(out=xnT[:, :, blk * bs:(blk + 1) * bs],
                                                              in_=pv.rearrange("p (k t) -> p k t", t=128)[:, :, 0:bs]),
                      reads=pk(pb), writes=[("xnT", blk)])
                if blk > 0:
                    tm_block(blk - 1)
            tm_block(nb - 1)
            res = {"xnT": xnT, "vst": vst}

            def part2():
                XNT = [("xnT", blk) for blk in range(nb)]
                pcount = [0]

                def fm_proj(col0, evac):
                    pb = pcount[0] % 4
                    pcount[0] += 1
                    for kc in range(8):
                        S.pe(lambda e, kc=kc, pb=pb: e.matmul(bank(pb)[:, 0:NT], lhsT=w_in_bf[:, kc, col0:col0 + 128], rhs=xnT[:, kc, 0:NT],
                                                              start=(kc == 0), stop=(kc == 7)),
                             reads=XNT + W_IN_K, writes=pk(pb))
                    evac(pb)

                def tm_proj(blk, col0, evac):
                    pb = pcount[0] % 4
                    pcount[0] += 1
                    for kc in range(8):
                        S.pe(lambda e, kc=kc, pb=pb: e.matmul(bank(pb)[0:bs, :], lhsT=xnT[:, kc, blk * bs:(blk + 1) * bs], rhs=w_in_bf[:, kc, col0:col0 + 512],
                                                              start=(kc == 0), stop=(kc == 7)),
                             reads=XNT + W_IN_K, writes=pk(pb))
                    evac(pb)

                for c in range(4):
                    if smp:
                        def ev(pb, c=c):
                            S.act(lambda e: e.activation(out=u_s[:, c, :, 3:7], in_=bank(pb)[:, 0:16].rearrange("p (s t) -> p s t", t=4), func=AF.Copy),
                                  reads=pk(pb), writes=[("u_s", c)])
                    else:
                        def ev(pb, c=c):
                            S.act(lambda e: e.activation(out=u_ext2[ub][:, c, 3:3 + SUB], in_=bank(pb)[:, 0:SUB], func=AF.Copy),
                                  reads=pk(pb), writes=[("u_ext", ub, c)])
                    fm_proj(c * 128, ev)
                if not smp:
                    for c in range(4):
                        L1_pre(c, ub)
                if mode == "own":
                    for c in range(4):
                        def ev(pb, c=c):
                            S.act(lambda e: e.activation(out=gg[:, c, 0:NT], in_=bank(pb)[:, 0:NT], func=AF.Gelu_apprx_tanh),
                                  reads=pk(pb), writes=[("gg", c)])
                        fm_proj(512 + c * 128, ev)
                    for c in range(4):
                        def ev(pb, c=c):
                            S.dve(lambda e: e.tensor_copy(out=qT[:, c, 0:NT], in_=bank(pb)[:, 0:NT]), reads=pk(pb), writes=[("qT", c)])
                        fm_proj(1024 + c * 128, ev)
                        if not smp:
                            L1_post(c, ub)
                elif not smp:
                    for c in range(4):
                        L1_post(c, ub)
                if mode in ("own", "kv"):
                    res["kTs"] = kTs
                    for c in range(4):
                        def ev(pb, c=c):
                            S.dve(lambda e: e.tensor_copy(out=kTs[:, c, 0:NT], in_=bank(pb)[:, 0:NT]), reads=pk(pb), writes=[("kTs", c)])
                        fm_proj(1536 + c * 128, ev)
                        if not smp:
                            ld(Kscr[c * 128:(c + 1) * 128, scr_col:scr_col + SUB], kTs[:, c, :], ["Kscr"], r=[("kTs", c)], eng="pool")
                return res

            if split:
                return part2
            return part2()

        def L_bufs():
            ovl_pos[0] = L_OFF
            d = {}
            d["uc"] = ov([128, 4, SUB])
            d["ucb"] = ov([128, 4, SUB], BF16)
            d["rr"] = ov([128, 4, SUB])
            d["ii"] = ov([128, 4, SUB])
            d["tmp"] = [ov([128, SUB]) for _ in range(4)]
            d["hb"] = [ov([128, SUB]), ov([128, SUB])]
            return d

        def L1_pre(c, ub):
            d = L_bufs()
            uc, ucb = d["uc"], d["ucb"]
            u_ext = u_ext2[ub]
            S.dve(lambda e: e.tensor_scalar(out=uc[:, c, :], in0=u_ext[:, c, 3:3 + SUB], scalar1=cw[:, c, 3:4], scalar2=cb[:, c:c + 1], op0=ALU.mult, op1=ALU.add),
                  reads=[("u_ext", ub, c), ("u_carry", ub), "cb"] + CWK, writes=[("uc", c)])
            for j in (2, 1, 0):
                S.dve(lambda e, j=j: e.scalar_tensor_tensor(out=uc[:, c, :], in0=u_ext[:, c, j:j + SUB], scalar=cw[:, c, j:j + 1], in1=uc[:, c, :], op0=ALU.mult, op1=ALU.add),
                      reads=[("u_ext", ub, c), ("u_carry", ub), ("uc", c)] + CWK, writes=[("uc", c)])
            S.act(lambda e: e.activation(out=ucb[:, c, :], in_=uc[:, c, :], func=AF.Copy), reads=[("uc", c)], writes=[("ucb", c)])

        def L1_post(c, ub):
            d = L_bufs()
            ucb, rr, ii = d["ucb"], d["rr"], d["ii"]
            S.pe(lambda e: e.matmul(bank(4)[:, 0:SUB], lhsT=Wa_bd[:, c, :], rhs=ucb[:, c, :], start=True, stop=True), reads=[("ucb", c), "Wa_bd"], writes=pk(4))
            S.pe(lambda e: e.matmul(bank(5)[:, 0:SUB], lhsT=Wx_bd[:, c, :], rhs=ucb[:, c, :], start=True, stop=True), reads=[("ucb", c), "Wx_bd"], writes=pk(5))
            S.act(lambda e: e.activation(out=rr[:, c, :], in_=bank(4)[:, 0:SUB], func=AF.Sigmoid, bias=b_a[:, c:c + 1], scale=1.0), reads=pk(4) + ["b_a"], writes=[("rr", c)])
            S.act(lambda e: e.activation(out=ii[:, c, :], in_=bank(5)[:, 0:SUB], func=AF.Sigmoid, bias=b_x[:, c:c + 1], scale=1.0), reads=pk(5) + ["b_x"], writes=[("ii", c)])

        def L2_gen(own, last=False, ub=0):
            d = L_bufs()
            uc, rr, ii, tmp, hb = d["uc"], d["rr"], d["ii"], d["tmp"], d["hb"]
            u_ext = u_ext2[ub]
            for c in range(4):
                S.act(lambda e, c=c: e.activation(out=rr[:, c, :], in_=rr[:, c, :], func=AF.Exp, scale=cvec[:, c:c + 1]), reads=[("rr", c), "cvec"], writes=[("rr", c)])
                yield
            for c in range(4):
                S.act(lambda e, c=c: e.activation(out=tmp[c], in_=rr[:, c, :], func=AF.Square), reads=[("rr", c)], writes=[("tmp", c)])
            for c in range(4):
                S.act(lambda e, c=c: e.activation(out=tmp[c], in_=tmp[c], func=AF.Sqrt, scale=-1.0, bias=one_t[:, 0:1]), reads=[("tmp", c), "one_t"], writes=[("tmp", c)])
            yield
            for c in range(4):
                t = tmp[c]
                tk = ("tmp", c)
                h = hb[c % 2]
                hk = ("hb", c % 2)
                S.dve(lambda e, c=c, t=t: e.tensor_tensor(out=ii[:, c, :], in0=ii[:, c, :], in1=t, op=ALU.mult), reads=[("ii", c), tk], writes=[("ii", c)])
                yield
                S.dve(lambda e, c=c: e.tensor_tensor(out=ii[:, c, :], in0=ii[:, c, :], in1=uc[:, c, :], op=ALU.mult), reads=[("ii", c), ("uc", c)], writes=[("ii", c)])
                yield
                S.dve(lambda e, c=c, h=h: e.tensor_tensor_scan(out=h, data0=rr[:, c, :], data1=ii[:, c, :], initial=hstate[:, c:c + 1], op0=ALU.mult, op1=ALU.add),
                      reads=[("rr", c), ("ii", c), "hstate"], writes=[hk])
                S.dve(lambda e, c=c, h=h: e.tensor_copy(out=hstate[:, c:c + 1], in_=h[:, SUB - 1:SUB]), reads=[hk], writes=["hstate"])
                yield
                if own:
                    S.pool(lambda e, c=c, h=h: e.tensor_tensor(out=ylru[:, c, :], in0=h, in1=gg[:, c, :], op=ALU.mult), reads=[hk, ("gg", c)], writes=[("ylru", c)])
            if last:
                LAST_UB[0] = ub
            S.dve(lambda e: e.tensor_copy(out=u_ext2[1 - ub][:, :, 0:3], in_=u_ext[:, :, SUB:SUB + 3]),
                  reads=[("u_ext", ub, c) for c in range(4)], writes=[("u_carry", 1 - ub)])
            yield

        def finalize_head(h, ob, NTq, otmp, rz):
            c = h // 2
            po = (h % 2) * 64
            S.dve(lambda e: e.reciprocal(out=rz[64:65, 0:NTq], in_=bank(ob)[64:65, 0:NTq]), reads=pk(ob), writes=["rz"])
            S.dve(lambda e: e.tensor_copy(out=otmp[0:64, 0:NTq], in_=bank(ob)[0:64, 0:NTq]), reads=pk(ob), writes=["otmp"])
            S.pe(lambda e: e.matmul(bank(6)[0:64, 0:NTq], lhsT=ones_f[64:65, 0:64], rhs=rz[64:65, 0:NTq], start=True, stop=True),
                 reads=["rz", "ones_f"], writes=pk(6))
            S.dve(lambda e: e.tensor_tensor(out=yatt[po:po + 64, c, 0:NTq], in0=otmp[0:64, 0:NTq], in1=bank(6)[0:64, 0:NTq], op=ALU.mult),
                  reads=["otmp"] + pk(6), writes=[("yatt", h)])

        def stage_B(s, hook=None):
            ovl_reset()
            KT = ov([128, 4, KVH + SUB], BF16)
            V1 = ov([128, 5, VW], BF16)
            V4x = ov([128, 20, VW], BF16)
            E1 = ov([128, 1024], BF16)
            Ex = [ov([128, 512], BF16) for _ in range(4)]
            EF = ov([128, 512], BF16)
            otmp = ov([128, SUB])
            rz = ov([128, SUB])
            c0 = s * SUB
            ld(KT[:, 0, :], Kscr[0:128, c0:c0 + KVH + SUB], [("KT", 0)], r=["Kscr"])
            R0 = KVH + s * SUB
            ld(V1, Vscr[R0 - 128:R0 + 512, :].rearrange("(b p) c -> p b c", p=128), ["V1"], r=["Vscr"])
            for r in range(4):
                src = Vscr[R0 - 2048:R0 + 512, :].rearrange("(b j q) c -> q j b c", b=5, q=4)[r]
                ld(V4x[:, 5 * r:5 * r + 5, :], src, [("V4x", r)], r=["Vscr"])
            for c in range(1, 4):
                ld(KT[:, c, :], Kscr[c * 128:(c + 1) * 128, c0:c0 + KVH + SUB], [("KT", c)], r=["Kscr"])
            WQ = KVH

            def hv(h):
                c = h // 2
                po = (h % 2) * 64
                return c, po, qT[po:po + 64, c, :], KT[po:po + 64, c, :], 4 + (h % 2), slice(h * 65, (h + 1) * 65), [("qT", c)]

            def tiles1():
                col = 0
                out = []
                for kb in range(5):
                    q0 = max(0, 128 * (kb - 1))
                    q1 = min(SUB, 128 * (kb + 1))
                    out.append((kb, q0, q1, col))
                    col += q1 - q0
                return out

            def QK1(h):
                c, po, qh, kh, ob, hs, qk = hv(h)
                for (kb, q0, q1, col) in tiles1():
                    kcol = WQ - 128 + 128 * kb
                    for qq in range(q0, q1, 128):
                        cc = col + (qq - q0)
                        S.pe(lambda e, kcol=kcol, qq=qq, cc=cc: e.matmul(psum[:, cc:cc + 128], lhsT=kh[:, kcol:kcol + 128], rhs=qh[:, qq:qq + 128], start=True, stop=True),
                             reads=[("KT", c)] + qk, writes=pk(0, 2))

            def X1(h):
                for hb_ in range(2):
                    S.act(lambda e, hb_=hb_: e.activation(out=E1[:, hb_ * 512:(hb_ + 1) * 512], in_=bank(hb_), func=AF.Exp, scale=0.125),
                          reads=pk(hb_), writes=[("E1", hb_)])
                    S.dve(lambda e, hb_=hb_: e.tensor_tensor(out=E1[:, hb_ * 512:(hb_ + 1) * 512], in0=E1[:, hb_ * 512:(hb_ + 1) * 512], in1=m1[:, hb_ * 512:(hb_ + 1) * 512], op=ALU.mult),
                          reads=[("E1", hb_), "m1"], writes=[("E1", hb_)])

            def PV1(h):
                c, po, qh, kh, ob, hs, qk = hv(h)
                for (kb, q0, q1, col) in tiles1():
                    for qt in range(q0 // 128, q1 // 128):
                        pc = col + (qt * 128 - q0)
                        S.pe(lambda e, kb=kb, qt=qt, pc=pc: e.matmul(bank(ob)[0:65, qt * 128:(qt + 1) * 128], lhsT=V1[:, kb, hs], rhs=E1[:, pc:pc + 128],
                                                                     start=(kb == 0), stop=False, skip_group_check=True),
                             reads=["V1", ("E1", 0), ("E1", 1)], writes=pk(ob))

            def QKx(h, r):
                c, po, qh, kh, ob, hs, qk = hv(h)
                bx_ = 2 + (r % 2)
                for kb in range(5):
                    kcol = 512 * kb + r
                    if kb == 0:
                        out = bank(6)[:, r * 128:(r + 1) * 128]
                        wk = pk(6)
                    else:
                        out = bank(bx_)[:, (kb - 1) * 128:kb * 128]
                        wk = pk(bx_)
                    S.pe(lambda e, kcol=kcol, out=out: e.matmul(out, lhsT=kh[:, kcol:kcol + 509:4], rhs=qh[:, r:SUB:4], start=True, stop=True),
                         reads=[("KT", c)] + qk, writes=wk)

            def Xx(h, r):
                bx_ = 2 + (r % 2)
                S.act(lambda e: e.activation(out=Ex[r][:, :], in_=bank(bx_), func=AF.Exp, scale=0.125), reads=pk(bx_), writes=[("Ex", r)])
                S.dve(lambda e: e.tensor_tensor(out=Ex[r][:, :], in0=Ex[r][:, :], in1=mxa[:, :], op=ALU.mult), reads=[("Ex", r), "mxa"], writes=[("Ex", r)])

            def XF(h):
                S.act(lambda e: e.activation(out=EF[:, :], in_=bank(6), func=AF.Exp, scale=0.125), reads=pk(6), writes=["EF"])
                S.dve(lambda e: e.tensor_tensor(out=EF[:, :], in0=EF[:, :], in1=mxf[:, :], op=ALU.mult), reads=["EF", "mxf"], writes=["EF"])

            def PVx(h, r):
                c, po, qh, kh, ob, hs, qk = hv(h)
                for kb in range(1, 5):
                    S.pe(lambda e, kb=kb: e.matmul(bank(ob)[0:65, r:SUB:4], lhsT=V4x[:, 5 * r + kb, hs], rhs=Ex[r][:, (kb - 1) * 128:kb * 128],
                                                   start=False, stop=False, skip_group_check=True),
                         reads=[("V4x", r), ("Ex", r)], writes=pk(ob))

            def PVF(h):
                c, po, qh, kh, ob, hs, qk = hv(h)
                for r in range(4):
                    S.pe(lambda e, r=r: e.matmul(bank(ob)[0:65, r:SUB:4], lhsT=V4x[:, 5 * r, hs], rhs=EF[:, r * 128:(r + 1) * 128],
                                                 start=False, stop=(r == 3), skip_group_check=True),
                         reads=[("V4x", r), "EF"], writes=pk(ob))

            def FIN_a(h):
                c, po, qh, kh, ob, hs, qk = hv(h)
                S.dve(lambda e: e.reciprocal(out=rz[64:65, 0:SUB], in_=bank(ob)[64:65, 0:SUB]), reads=pk(ob), writes=["rz"])
                S.act(lambda e: e.activation(out=otmp[0:64, 0:SUB], in_=bank(ob)[0:64, 0:SUB], func=AF.Copy), reads=pk(ob), writes=["otmp"])

            def FIN_b(h):
                c, po, qh, kh, ob, hs, qk = hv(h)
                S.pe(lambda e: e.matmul(bank(7)[0:64, 0:SUB], lhsT=ones_f[64:65, 0:64], rhs=rz[64:65, 0:SUB], start=True, stop=True),
                     reads=["rz", "ones_f"], writes=pk(7))
                S.dve(lambda e: e.tensor_tensor(out=yatt[po:po + 64, c, 0:SUB], in0=otmp[0:64, 0:SUB], in1=bank(7)[0:64, 0:SUB], op=ALU.mult),
                      reads=["otmp"] + pk(7), writes=[("yatt", h)])

            QK1(0); X1(0)
            for h in range(NH):
                hk_ = hook if hook else (lambda: None)
                QKx(h, 0); Xx(h, 0); hk_(); QKx(h, 1); Xx(h, 1)
                if h > 0:
                    FIN_b(h - 1)
                hk_()
                PV1(h)
                QKx(h, 2); Xx(h, 2); hk_(); QKx(h, 3); Xx(h, 3); XF(h)
                hk_()
                if h + 1 < NH:
                    QK1(h + 1); X1(h + 1)
                for r in range(4):
                    PVx(h, r)
                PVF(h)
                FIN_a(h)
            FIN_b(NH - 1)

        def stage_C(src, row0, NT, dst, drow0):
            ovl_reset()
            bs = min(128, NT)
            nb = NT // bs
            xblk = [ov([128, D]), ov([128, D])]
            x1 = ov([128, 4, D])
            xn2T = ov([128, 8, SUB], BF16)
            hmid = ov([128, NFF, SUB], BF16)
            ysl = ov([128, 4, SUB], BF16)
            ysa = ov([128, 4, SUB], BF16)
            wgr = [ov([128, 8, 128], BF16) for _ in range(3)]
            wur = [ov([128, 8, 128], BF16) for _ in range(3)]
            wdr = [ov([128, D], BF16) for _ in range(4)]
            sg = [ov([128, SUB]), ov([128, SUB])]
            xnb = [ov([128, D], BF16), ov([128, D], BF16)]
            junk = ov([128, D], BF16)
            YL = [("ylru", c) for c in range(4)]
            YA = [("yatt", h) for h in range(8)]
            S.act(lambda e: e.activation(out=ysl[:, :, 0:NT], in_=ylru[:, :, 0:NT], func=AF.Square), reads=YL, writes=["ysl"])
            S.dve(lambda e: e.tensor_tensor(out=ysa[:, :, 0:NT], in0=yatt[:, :, 0:NT], in1=yatt[:, :, 0:NT], op=ALU.mult), reads=YA, writes=["ysa"])
            def c_tr(blk):
                b = blk % 2
                ts = slice(blk * bs, (blk + 1) * bs)
                pb = 4 + b
                pv = bankb(pb)
                for kc in range(8):
                    S.pe(lambda e, kc=kc: e.transpose(pv[:, kc * 128: kc * 128 + bs], xnb[b][0:bs, kc * 128:(kc + 1) * 128], ident_b[0:bs, 0:bs]),
                         reads=[("xnb", b), "ident_b"], writes=pk(pb))
                S.act(lambda e: e.activation(out=xn2T[:, :, ts], in_=pv.rearrange("p (k t) -> p k t", t=128)[:, :, 0:bs], func=AF.Copy),
                      reads=pk(pb), writes=[("xn2T", blk)])

            for blk in range(nb):
                b = blk % 2
                ts = slice(blk * bs, (blk + 1) * bs)
                ld(xblk[b][0:bs], src[row0 + blk * bs: row0 + (blk + 1) * bs, :], [("xblk", b)])
                for c in range(4):
                    S.pe(lambda e, c=c, ts=ts: e.matmul(bank(6)[0:bs, 0:1], lhsT=ysl[:, c, ts], rhs=ones_b[:, 0:1], start=(c == 0), stop=(c == 3), skip_group_check=True),
                         reads=["ysl", "ones_b"], writes=pk(6))
                for c in range(4):
                    S.pe(lambda e, c=c, ts=ts: e.matmul(bank(6)[0:bs, 1:2], lhsT=ysa[:, c, ts], rhs=ones_b[:, 0:1], start=(c == 0), stop=(c == 3), skip_group_check=True),
                         reads=["ysa", "ones_b"], writes=pk(6))
                S.act(lambda e: e.activation(out=ssq[0:bs, 2:4], in_=bank(6)[0:bs, 0:2], func=AF.Sqrt, scale=1.0 / DL, bias=eps_t[0:bs, 0:1]),
                      reads=pk(6) + ["eps_t"], writes=["ssq2"])
                S.dve(lambda e: e.reciprocal(out=rstd[0:bs, 2:4], in_=ssq[0:bs, 2:4]), reads=["ssq2"], writes=["rstd2"])
                for half in range(2):
                    for c in range(4):
                        S.pe(lambda e, c=c, ts=ts, half=half: e.matmul(bank(half)[0:bs, :], lhsT=ylru[:, c, ts], rhs=w_out_bf[:, c, half * 512:(half + 1) * 512],
                                                                       start=(c == 0), stop=(c == 3)),
                             reads=YL + W_OUT_K, writes=pk(half))
                for half in range(2):
                    for c in range(4):
                        S.pe(lambda e, c=c, ts=ts, half=half: e.matmul(bank(2 + half)[0:bs, :], lhsT=yatt[:, c, ts], rhs=w_out_bf[:, 4 + c, half * 512:(half + 1) * 512],
                                                                       start=(c == 0), stop=(c == 3)),
                             reads=YA + W_OUT_K, writes=pk(2 + half))
                S.dve(lambda e, b=b, blk=blk: e.scalar_tensor_tensor(out=x1[0:bs, blk, :], in0=psum[0:bs, 0:1024], scalar=rstd[0:bs, 2:3], in1=xblk[b][0:bs],
                                                                     op0=ALU.mult, op1=ALU.add),
                      reads=pk(0, 2) + ["rstd2", ("xblk", b)], writes=[("x1", blk)])
                S.dve(lambda e, blk=blk: e.scalar_tensor_tensor(out=x1[0:bs, blk, :], in0=psum[0:bs, 1024:2048], scalar=rstd[0:bs, 3:4], in1=x1[0:bs, blk, :],
                                                                op0=ALU.mult, op1=ALU.add),
                      reads=pk(2, 2) + ["rstd2", ("x1", blk)], writes=[("x1", blk)])
                S.dve(lambda e: e.memset(ssq[0:bs, 4:5], 0.0), writes=["ssq3"])
                S.act(lambda e, blk=blk: e.activation(out=junk[0:bs], in_=x1[0:bs, blk, :], func=AF.Square, accum_out=ssq[0:bs, 4:5]),
                      reads=[("x1", blk)], writes=["junk", "ssq3"])
                S.act(lambda e: e.activation(out=ssq[0:bs, 4:5], in_=ssq[0:bs, 4:5], func=AF.Sqrt, scale=1.0 / D, bias=eps_t[0:bs, 0:1]),
                      reads=["ssq3", "eps_t"], writes=["ssq3"])
                S.dve(lambda e: e.reciprocal(out=rstd[0:bs, 4:5], in_=ssq[0:bs, 4:5]), reads=["ssq3"], writes=["rstd3"])
                S.dve(lambda e, b=b, blk=blk: e.tensor_scalar(out=xnb[b][0:bs], in0=x1[0:bs, blk, :], scalar1=rstd[0:bs, 4:5], scalar2=None, op0=ALU.mult),
                      reads=[("x1", blk), "rstd3"], writes=[("xnb", b)])
                if blk > 0:
                    c_tr(blk - 1)
            c_tr(nb - 1)
            XN2 = [("xn2T", blk) for blk in range(nb)]
            for f in range(NFF):
                rb = f % 3
                ld(wgr[rb], wg_s[f], [("wgr", rb)], r=["ffn_scr"])
                ld(wur[rb], wu_s[f], [("wur", rb)], r=["ffn_scr"])
                pg = (f % 2) * 2
                for kc in range(8):
                    S.pe(lambda e, kc=kc, rb=rb, pg=pg: e.matmul(bank(pg)[:, 0:NT], lhsT=wgr[rb][:, kc, :], rhs=xn2T[:, kc, 0:NT], start=(kc == 0), stop=(kc == 7)),
                         reads=XN2 + [("wgr", rb)], writes=pk(pg))
                for kc in range(8):
                    S.pe(lambda e, kc=kc, rb=rb, pg=pg: e.matmul(bank(pg + 1)[:, 0:NT], lhsT=wur[rb][:, kc, :], rhs=xn2T[:, kc, 0:NT], start=(kc == 0), stop=(kc == 7)),
                         reads=XN2 + [("wur", rb)], writes=pk(pg + 1))
                sgb = sg[f % 2]
                S.act(lambda e, pg=pg, sgb=sgb: e.activation(out=sgb[:, 0:NT], in_=bank(pg)[:, 0:NT], func=AF.Silu), reads=pk(pg), writes=[("sg", f % 2)])
                S.dve(lambda e, f=f, pg=pg, sgb=sgb: e.tensor_tensor(out=hmid[:, f, 0:NT], in0=sgb[:, 0:NT], in1=bank(pg + 1)[:, 0:NT], op=ALU.mult),
                      reads=[("sg", f % 2)] + pk(pg + 1), writes=[("hmid", f)])
            HM = [("hmid", f) for f in range(NFF)]
            dcnt = 0
            for p0 in range(0, nb, 2):
                blks = list(range(p0, min(nb, p0 + 2)))
                for f in range(NFF):
                    rb = dcnt % 4
                    dcnt += 1
                    ld(wdr[rb], wd_s[f], [("wdr", rb)], r=["ffn_scr"])
                    for bi, blk in enumerate(blks):
                        ts = slice(blk * bs, (blk + 1) * bs)
                        for half in range(2):
                            pb = bi * 2 + half
                            S.pe(lambda e, f=f, ts=ts, half=half, pb=pb, rb=rb: e.matmul(bank(pb)[0:bs, :], lhsT=hmid[:, f, ts], rhs=wdr[rb][:, half * 512:(half + 1) * 512],
                                                                                         start=(f == 0), stop=(f == NFF - 1)),
                                 reads=HM + [("wdr", rb)], writes=pk(pb))
                for bi, blk in enumerate(blks):
                    b = blk % 2
                    yf = xblk[b]
                    S.dve(lambda e, bi=bi, blk=blk, yf=yf: e.tensor_tensor(out=yf[0:bs], in0=psum[0:bs, bi * 1024:(bi + 1) * 1024], in1=x1[0:bs, blk, :], op=ALU.add),
                          reads=pk(bi * 2, 2) + [("x1", blk)], writes=[("xblk", b)])
                    S.dve(lambda e: e.memset(ssq[0:bs, 5:6], 0.0), writes=["ssq4"])
                    S.act(lambda e, yf=yf: e.activation(out=junk[0:bs], in_=yf[0:bs], func=AF.Square, accum_out=ssq[0:bs, 5:6]),
                          reads=[("xblk", b)], writes=["junk", "ssq4"])
                    S.act(lambda e: e.activation(out=ssq[0:bs, 5:6], in_=ssq[0:bs, 5:6], func=AF.Sqrt, scale=1.0 / D, bias=eps_t[0:bs, 0:1]),
                          reads=["ssq4", "eps_t"], writes=["ssq4"])
                    S.dve(lambda e: e.reciprocal(out=rstd[0:bs, 5:6], in_=ssq[0:bs, 5:6]), reads=["ssq4"], writes=["rstd4"])
                    S.dve(lambda e, yf=yf: e.scalar_tensor_tensor(out=yf[0:bs], in0=yf[0:bs], scalar=rstd[0:bs, 5:6], in1=gfin[0:bs], op0=ALU.mult, op1=ALU.mult),
                          reads=[("xblk", b), "rstd4", "gfin"], writes=[("xblk", b)])
                    ld(dst[drow0 + blk * bs: drow0 + (blk + 1) * bs, :], yf[0:bs], ["o_y"], r=[("xblk", b)], eng="pool", nobar=(NT == SUB))


        def stage_L_smp():
            ucs = ov([128, 4, 16])
            ucb = ov([128, 4, 16], BF16)
            rr = ov([128, 4, 16])
            ii = ov([128, 4, 16])
            tmp = ov([128, 4, 16])
            hs = hs_s
            for c in range(4):
                uv = ucs[:, c, :].rearrange("p (s t) -> p s t", t=4)
                S.dve(lambda e, c=c, uv=uv: e.tensor_scalar(out=uv, in0=u_s[:, c, :, 3:7], scalar1=cw[:, c, 3:4], scalar2=cb[:, c:c + 1], op0=ALU.mult, op1=ALU.add),
                      reads=[("u_s", c), "cb"] + [("u_s0", c, q) for q in range(4)] + CWK, writes=[("ucs", c)])
                for j in (2, 1, 0):
                    S.dve(lambda e, c=c, j=j, uv=uv: e.scalar_tensor_tensor(out=uv, in0=u_s[:, c, :, j:j + 4], scalar=cw[:, c, j:j + 1], in1=uv, op0=ALU.mult, op1=ALU.add),
                          reads=[("u_s", c), ("ucs", c)] + [("u_s0", c, q) for q in range(4)] + CWK, writes=[("ucs", c)])
                S.act(lambda e, c=c: e.activation(out=ucb[:, c, :], in_=ucs[:, c, :], func=AF.Copy), reads=[("ucs", c)], writes=[("ucbs", c)])
                S.pe(lambda e, c=c: e.matmul(bank(4)[:, 0:16], lhsT=Wa_bd[:, c, :], rhs=ucb[:, c, :], start=True, stop=True), reads=[("ucbs", c), "Wa_bd"], writes=pk(4))
                S.pe(lambda e, c=c: e.matmul(bank(5)[:, 0:16], lhsT=Wx_bd[:, c, :], rhs=ucb[:, c, :], start=True, stop=True), reads=[("ucbs", c), "Wx_bd"], writes=pk(5))
                S.act(lambda e, c=c: e.activation(out=rr[:, c, :], in_=bank(4)[:, 0:16], func=AF.Sigmoid, bias=b_a[:, c:c + 1], scale=1.0), reads=pk(4) + ["b_a"], writes=[("rrs", c)])
                S.act(lambda e, c=c: e.activation(out=ii[:, c, :], in_=bank(5)[:, 0:16], func=AF.Sigmoid, bias=b_x[:, c:c + 1], scale=1.0), reads=pk(5) + ["b_x"], writes=[("iis", c)])
            for c in range(4):
                S.act(lambda e, c=c: e.activation(out=rr[:, c, :], in_=rr[:, c, :], func=AF.Exp, scale=cvec[:, c:c + 1]), reads=[("rrs", c), "cvec"], writes=[("rrs", c)])
            for c in range(4):
                S.act(lambda e, c=c: e.activation(out=tmp[:, c, :], in_=rr[:, c, :], func=AF.Square), reads=[("rrs", c)], writes=[("tmps", c)])
                S.act(lambda e, c=c: e.activation(out=tmp[:, c, :], in_=tmp[:, c, :], func=AF.Sqrt, scale=-1.0, bias=one_t[:, 0:1]), reads=[("tmps", c), "one_t"], writes=[("tmps", c)])
                S.dve(lambda e, c=c: e.tensor_tensor(out=ii[:, c, :], in0=ii[:, c, :], in1=tmp[:, c, :], op=ALU.mult), reads=[("iis", c), ("tmps", c)], writes=[("iis", c)])
                S.dve(lambda e, c=c: e.tensor_tensor(out=ii[:, c, :], in0=ii[:, c, :], in1=ucs[:, c, :], op=ALU.mult), reads=[("iis", c), ("ucs", c)], writes=[("iis", c)])
                for q in range(4):
                    S.dve(lambda e, c=c, q=q: e.tensor_tensor_scan(out=hs[:, c, q * 4:(q + 1) * 4], data0=rr[:, c, q * 4:(q + 1) * 4], data1=ii[:, c, q * 4:(q + 1) * 4],
                                                                   initial=h0_s[:, c, q:q + 1], op0=ALU.mult, op1=ALU.add),
                          reads=[("rrs", c), ("iis", c), ("h0_s", c)], writes=[("hs", c, q)])
                HK = [("hs", c, q) for q in range(4)]
                S.dve(lambda e, c=c: e.tensor_tensor(out=ylru[:, c, 0:16], in0=hs[:, c, :], in1=gg[:, c, 0:16], op=ALU.mult), reads=HK + [("gg", c)], writes=[("ylru", c)])

        def stage_B_smp(resA):
            xnT = resA["xnT"]
            kTs = resA["kTs"]
            kc_f = ov([128, 8, 512])
            vc_f = ov([128, 8, 512])
            KTc = ov([128, 8, 4, 128], BF16)
            vcs = ov([128, 8, VW], BF16)
            vN = ov([128, 4, VW], BF16)
            Es = ov([128, 288], BF16)
            otmp = ov([128, 128])
            rz = ov([128, 128])
            S.dve(lambda e: e.memset(vcs, 1.0), writes=["vcs"])
            S.dve(lambda e: e.memset(vN, 1.0), writes=["vN"])
            first = [True]
            for q in range(4):
                for kc in range(8):
                    S.pe(lambda e, kc=kc, q=q: e.matmul(bank(7)[0:4, :], lhsT=xnT[:, kc, q * 4:(q + 1) * 4], rhs=w_in_bf[:, kc, 2048:2560], start=(kc == 0), stop=(kc == 7)),
                         reads=[("xnT", 0)] + W_IN_K, writes=pk(7))
                S.act(lambda e, q=q: e.activation(out=vN[0:4, q, :].rearrange("p (h d) -> p h d", d=65)[:, :, 0:64],
                                                  in_=bank(7)[0:4, :].rearrange("p (h d) -> p h d", d=64), func=AF.Copy),
                      reads=pk(7) + ["vN"], writes=["vN"])
                for b in range(4):
                    ld(kc_f[:, b, :], ck[q, 1536 + 128 * b:1536 + 128 * (b + 1), :], [("kc_f", b)])
                    ld(vc_f[:, b, :], cv[q, 1536 + 128 * b:1536 + 128 * (b + 1), :], [("vc_f", b)])
                    ld(kc_f[:, 4 + b, :], ck[q, b:2048:16, :], [("kc_f", 4 + b)])
                    ld(vc_f[:, 4 + b, :], cv[q, b:2048:16, :], [("vc_f", 4 + b)])
                for blk in range(8):
                    tb = 2 + blk % 2
                    for c in range(4):
                        S.pe(lambda e, blk=blk, c=c, tb=tb: e.transpose(bank(tb)[:, c * 128:(c + 1) * 128], kc_f[:, blk, c * 128:(c + 1) * 128], ident_f[:]),
                             reads=[("kc_f", blk), "ident_f"], writes=pk(tb))
                    S.act(lambda e, blk=blk, tb=tb: e.activation(out=KTc[:, blk, :, :], in_=bank(tb).rearrange("p (c k) -> p c k", k=128), func=AF.Copy),
                          reads=pk(tb), writes=[("KTc", blk)])
                    S.dve(lambda e, blk=blk: e.tensor_copy(out=vcs[:, blk, :].rearrange("p (h d) -> p h d", d=65)[:, :, 0:64],
                                                           in_=vc_f[:, blk, :].rearrange("p (h d) -> p h d", d=64)),
                          reads=[("vc_f", blk), "vcs"], writes=[("vcs", blk)])
                KSUB = int(os.environ.get("KSUB", "9"))
                if KSUB <= 1:
                    break
                for blk in range(8):
                    for h in range(NH):
                        c, po, par = h // 2, (h % 2) * 64, h % 2
                        col = (blk * 4 + h // 2) * 4
                        S.pe(lambda e, blk=blk, c=c, po=po, col=col, q=q, par=par: e.matmul(bank(par)[:, col:col + 4], lhsT=KTc[po:po + 64, blk, c, :], rhs=qT[po:po + 64, c, q * 4:(q + 1) * 4],
                                                                                           start=True, stop=True),
                             reads=[("KTc", blk), ("qT", c)], writes=pk(par))
                for h in range(NH):
                    c, po, par = h // 2, (h % 2) * 64, h % 2
                    col = (32 + h // 2) * 4
                    S.pe(lambda e, c=c, po=po, col=col, q=q, par=par: e.matmul(bank(par)[0:4, col:col + 4], lhsT=kTs[po:po + 64, c, q * 4:(q + 1) * 4], rhs=qT[po:po + 64, c, q * 4:(q + 1) * 4],
                                                                              start=True, stop=True),
                         reads=[("kTs", c), ("qT", c)], writes=pk(par))
                for par in range(2):
                    S.act(lambda e, par=par: e.activation(out=Es[:, par * 144:(par + 1) * 144], in_=bank(par)[:, 0:144], func=AF.Exp, scale=0.125), reads=pk(par), writes=[("Es", par)])
                    for blk in range(9):
                        np_ = 4 if blk == 8 else 128
                        o0 = par * 144 + blk * 16
                        S.dve(lambda e, blk=blk, np_=np_, o0=o0: e.tensor_tensor(out=Es[0:np_, o0:o0 + 16].rearrange("p (h q) -> p h q", q=4),
                                                                                 in0=Es[0:np_, o0:o0 + 16].rearrange("p (h q) -> p h q", q=4),
                                                                                 in1=msmp[0:np_, blk * 4:(blk + 1) * 4].unsqueeze(1).to_broadcast([np_, 4, 4]), op=ALU.mult),
                              reads=[("Es", par), "msmp"], writes=[("Es", par)])
                if KSUB <= 2:
                    break
                for h in range(NH):
                    hsl = slice(h * 65, (h + 1) * 65)
                    oc = (q * 8 + h) * 4
                    par = h % 2
                    for blk in range(9):
                        col = par * 144 + (blk * 4 + h // 2) * 4
                        if blk < 8:
                            S.pe(lambda e, blk=blk, col=col, oc=oc, hsl=hsl, st_=first[0]: e.matmul(bank(4)[0:65, oc:oc + 4], lhsT=vcs[:, blk, hsl], rhs=Es[:, col:col + 4],
                                                                                                 start=st_, stop=False, skip_group_check=True),
                                 reads=[("vcs", blk), ("Es", par)], writes=pk(4))
                            first[0] = False
                        else:
                            S.pe(lambda e, col=col, oc=oc, hsl=hsl, q=q: e.matmul(bank(4)[0:65, oc:oc + 4], lhsT=vN[0:4, q, hsl], rhs=Es[0:4, col:col + 4],
                                                                                 start=False, stop=True, skip_group_check=True),
                                 reads=["vN", ("Es", par)], writes=pk(4))
            if KSUB <= 3:
                return
            S.dve(lambda e: e.reciprocal(out=rz[64:65, 0:128], in_=bank(4)[64:65, 0:128]), reads=pk(4), writes=["rz"])
            S.dve(lambda e: e.tensor_copy(out=otmp[0:64, 0:128], in_=bank(4)[0:64, 0:128]), reads=pk(4), writes=["otmp"])
            S.pe(lambda e: e.matmul(bank(6)[0:64, 0:128], lhsT=ones_f[64:65, 0:64], rhs=rz[64:65, 0:128], start=True, stop=True), reads=["rz", "ones_f"], writes=pk(6))
            for h in range(NH):
                c, po = h // 2, (h % 2) * 64
                S.dve(lambda e, h=h, c=c, po=po: e.tensor_tensor(out=yatt[po:po + 64, c, 0:16].rearrange("p (s q) -> p s q", q=4),
                                                                 in0=otmp[0:64, 0:128].rearrange("p (s h q) -> p s h q", h=8, q=4)[:, :, h, :],
                                                                 in1=bank(6)[0:64, 0:128].rearrange("p (s h q) -> p s h q", h=8, q=4)[:, :, h, :], op=ALU.mult),
                      reads=["otmp"] + pk(6), writes=[("yatt", h)])

        import os
        LIM = os.environ.get("KLIM", "all")
        nh = {"smpA": 0, "smpL": 0, "smpB": 0, "smp": 0, "setup": 0, "halo1": 5, "A": 0, "L": 0, "B": 0, "C": 0, "own2": 0, "own5": 0}.get(LIM, 8)
        hl = list(range(8 - nh, 8))
        gi = [0]

        def halo_A(s):
            mode = "kv" if s >= 4 else "lru"
            return stage_A(xs, s * SUB, SUB, mode, scr_col=(s - 4) * SUB if s >= 4 else None, ub=s % 2, split=True)

        if hl:
            halo_A(hl[0])()
        cgen = ffn_convert()
        for i, s in enumerate(hl):
            p2 = halo_A(hl[i + 1]) if i + 1 < len(hl) else None
            for yi, _ in enumerate(L2_gen(False, ub=s % 2)):
                if yi % 3 == 0:
                    next(cgen, None)
            if p2 is not None:
                p2()
        for _ in cgen:
            pass
        S.barrier()
        S.dve(lambda e: e.tensor_scalar(out=hstate[:], in0=hstate[:], scalar1=flag_t[:, 0:1], scalar2=None, op0=ALU.mult), reads=["hstate", "flag_t"], writes=["hstate"])
        nown = {"smpA": 0, "smpL": 0, "smpB": 0, "smp": 0, "setup": 0, "halo1": 0, "A": 1, "L": 1, "B": 1, "C": 1, "own1": 1, "own2": 2, "own5": 5, "h8o1": 1}.get(LIM, NSUB)
        for s in range(nown):
            stage_A(xs, HALO + s * SUB, SUB, "own", scr_col=KVH + s * SUB, win_row=(s - 4) * SUB if s >= 4 else None, ub=s % 2)
            if LIM == "A":
                break
            S.barrier()
            if LIM == "L":
                break
            l2 = L2_gen(True, last=(s == NSUB - 1), ub=s % 2)
            stage_B(s, hook=lambda: next(l2, None))
            for _ in l2:
                pass
            S.barrier()
            if DBG and s == 0:
                ld(d_yatt[:, :], yatt[:].rearrange("p c t -> p (c t)"), ["d_yatt"], r=[("yatt", h) for h in range(8)], eng="pool")
                ld(d_ylru[:, :], ylru[:].rearrange("p c t -> p (c t)"), ["d_ylru"], r=[("ylru", c) for c in range(4)], eng="pool")
                S.barrier()
            if LIM == "B":
                break
            stage_C(xs, HALO + s * SUB, SUB, o_y, s * SUB)
            S.barrier()

        if LIM in ("all", "smp", "smpA", "smpL", "smpB"):
            def smp_stores():
                ovl_reset()
                cst = ov([128, 4, 12])
                lst = ov([128, 4, 4])
                o1 = ov([128, 512]); o2 = ov([128, 512]); o3 = ov([128, 512]); o4 = ov([128, 512])
                ue = u_ext2[LAST_UB[0]]
                for c in range(4):
                    S.dve(lambda e, c=c: e.tensor_copy(out=cst[:, c, :].rearrange("p (q j) -> p q j", j=3), in_=u_s[:, c, :, 4:7]), reads=[("u_s", c)], writes=[("cst", c)])
                    S.dve(lambda e, c=c: e.tensor_copy(out=lst[:, c, :], in_=hs_s[:, c, 3:16:4]), reads=[("hs", c, q) for q in range(4)], writes=[("lst", c)])
                for c in range(4):
                    cs = slice(c * 128, (c + 1) * 128)
                    S.pe(lambda e, c=c, cs=cs: e.transpose(bank(0)[0:12, cs], cst[:, c, :], ident_f[:]), reads=[("cst", c), "ident_f"], writes=pk(0))
                    S.pe(lambda e, c=c, cs=cs: e.transpose(bank(1)[0:4, cs], lst[:, c, :], ident_f[:]), reads=[("lst", c), "ident_f"], writes=pk(1))
                    S.pe(lambda e, c=c, cs=cs: e.transpose(bank(2)[0:3, cs], ue[:, c, SUB:SUB + 3], ident_f[:]), reads=[("u_ext", LAST_UB[0], c), "ident_f"], writes=pk(2))
                    S.pe(lambda e, c=c, cs=cs: e.transpose(bank(3)[0:1, cs], hstate[:, c:c + 1], ident_f[:]), reads=["hstate", "ident_f"], writes=pk(3))
                S.act(lambda e: e.activation(out=o1[0:12], in_=bank(0)[0:12, :], func=AF.Copy), reads=pk(0), writes=["o1"])
                S.dve(lambda e: e.tensor_copy(out=o2[0:4], in_=bank(1)[0:4, :]), reads=pk(1), writes=["o2"])
                S.act(lambda e: e.activation(out=o3[0:3], in_=bank(2)[0:3, :], func=AF.Copy), reads=pk(2), writes=["o3"])
                S.dve(lambda e: e.tensor_copy(out=o4[0:1], in_=bank(3)[0:1, :]), reads=pk(3), writes=["o4"])
                ld(o_convs[:, :], o1[0:12], ["o_convs"], r=["o1"], eng="pool")
                ld(o_lrus[:, :], o2[0:4], ["o_lrus"], r=["o2"], eng="sp")
                ld(o_convp[:, :], o3[0:3], ["o_convp"], r=["o3"], eng="pool")
                ld(o_lrup[:, :], o4[0:1], ["o_lrup"], r=["o4"], eng="sp")

            resA = stage_A(x_smp, 0, 16, "own", win_row=0, smp=True)
            if LIM != "smpA":
                stage_L_smp()
                S.barrier()
                if LIM != "smpL":
                    stage_B_smp(resA)
                    S.barrier()
                    if LIM != "smpB":
                        stage_C(x_smp, 0, 16, o_ys, 0)
                        S.barrier()
            smp_stores()
        S.emit(nc, st)
    return nc


_NC_CACHE = {}


def kernel(**inputs):
    f32 = np.float32
    inp = {k: np.asarray(v) for k, v in inputs.items()}
    xp = inp["x_prompt"].astype(f32, copy=False)
    consts = _consts()
    if "nc" not in _NC_CACHE:
        _NC_CACHE["nc"] = build_nc()
    nc = _NC_CACHE["nc"]
    in_maps = []
    for c in range(8):
        b, half = c // 2, c % 2
        xs = np.zeros((HALO + TOK, D), f32)
        if half == 1:
            xs[:] = xp[b]
        else:
            xs[HALO:] = xp[b, :TOK]
        valid = np.ones((KVH + TOK,), f32)
        if half == 0:
            valid[:KVH] = 0.0
        m = {
            "xs": xs,
            "flag": np.full((128, 1), float(half), f32),
            "validc": np.ascontiguousarray(valid.reshape(48, 128).T),
            "x_smp": np.ascontiguousarray(inp["x_sample"][4 * c:4 * c + 4].reshape(16, D)),
            "sconv": np.ascontiguousarray(inp["state_conv"][0, 4 * c:4 * c + 4].reshape(12, DL)),
            "slru": np.ascontiguousarray(inp["state_lru"][0, 4 * c:4 * c + 4]),
            "ck": np.ascontiguousarray(inp["cache_k"][0, 4 * c:4 * c + 4].reshape(4, 2048, 512)),
            "cv": np.ascontiguousarray(inp["cache_v"][0, 4 * c:4 * c + 4].reshape(4, 2048, 512)),
            "norm_mix": inp["norm_mix"][0], "w_in": inp["w_in"][0], "conv_w": inp["conv_w"][0], "conv_b": inp["conv_b"][0],
            "lru_w_a": inp["lru_w_a"][0], "lru_b_a": inp["lru_b_a"][0], "lru_w_x": inp["lru_w_x"][0], "lru_b_x": inp["lru_b_x"][0],
            "lru_lambda": inp["lru_lambda"][0], "out_norm_lru": inp["out_norm_lru"][0], "out_norm_attn": inp["out_norm_attn"][0],
            "w_out": inp["w_out"][0], "norm_ffn": inp["norm_ffn"][0], "w_gate": inp["w_gate"][0], "w_up": inp["w_up"][0],
            "w_down": inp["w_down"][0], "norm_final": inp["norm_final"].reshape(1, D),
            "c_ident": consts["ident"], "c_m1": consts["m1"], "c_m4": consts["m4"], "c_m16": consts["m16"], "c_mxa": consts["mxa"], "c_mxf": consts["mxf"], "c_msmp": consts["msmp"],
        }
        in_maps.append({k: np.ascontiguousarray(v) for k, v in m.items()})
    res = run_bass_kernel_spmd(nc, in_maps, core_ids=list(range(8)))
    R = res.results
    _NC_CACHE['last'] = R
    y_prompt = np.zeros((4, 8192, D), f32)
    conv_p = np.zeros((1, 4, 3, DL), f32)
    lru_p = np.zeros((1, 4, DL), f32)
    kw = np.zeros((1, 4, 2048, NH, DH), f32)
    vw = np.zeros((1, 4, 2048, NH, DH), f32)
    y_s = np.zeros((32, 4, D), f32)
    conv_s = np.zeros((1, 32, 3, DL), f32)
    lru_s = np.zeros((1, 32, DL), f32)
    kn = np.zeros((1, 32, 4, NH, DH), f32)
    vn = np.zeros((1, 32, 4, NH, DH), f32)
    for c in range(8):
        b, half = c // 2, c % 2
        r = R[c]
        y_prompt[b, half * TOK:(half + 1) * TOK] = r["o_y"]
        if half == 1:
            conv_p[0, b] = r["o_convp"]
            lru_p[0, b] = r["o_lrup"][0]
            kw[0, b] = r["o_kwin"].reshape(2048, NH, DH)
            vw[0, b] = r["o_vwin"].reshape(2048, NH, DH)
        y_s[4 * c:4 * c + 4] = r["o_ys"].reshape(4, 4, D)
        conv_s[0, 4 * c:4 * c + 4] = r["o_convs"].reshape(4, 3, DL)
        lru_s[0, 4 * c:4 * c + 4] = r["o_lrus"]
        kn[0, 4 * c:4 * c + 4] = r["o_knew"].reshape(4, 4, NH, DH)
        vn[0, 4 * c:4 * c + 4] = r["o_vnew"].reshape(4, 4, NH, DH)
    return (y_prompt, y_s, conv_p, lru_p, kw, vw, conv_s, lru_s, kn, vn)
```
